# Optimizing a Trainium2 kernel written in Bass

```python
import jax, jax.numpy as jnp
from jax import lax
import numpy as np

D_MODEL = 1024
BATCH = 8
SEQ = 4096
DEPTH = 2

HEAD_DIM = 64
N_HEADS_FOX = 4
N_HEADS_NSA = 4
N_HEADS_SWA = 4
N_KV_SWA = 2
N_HEADS_DIL = 4
N_HEADS_MIX = N_HEADS_FOX + N_HEADS_NSA + N_HEADS_SWA + N_HEADS_DIL
D_MIX = N_HEADS_MIX * HEAD_DIM
D_FF = 2816
BLOCK = 128
NSA_CMP_LEN = 32
NSA_CMP_STRIDE = 16
NSA_CMP_HIDDEN = 128
NSA_SEL_LEN = 64
NSA_TOPN = 16
NSA_WINDOW = 512
SWA_WINDOW = 128
DIL_PAIRS = ((128, 1), (512, 4), (2048, 16))
RMS_EPS = 1e-6

IN_SPLITS = (
    ('fox_q', N_HEADS_FOX * HEAD_DIM), ('fox_k', N_HEADS_FOX * HEAD_DIM),
    ('fox_v', N_HEADS_FOX * HEAD_DIM), ('fox_f', N_HEADS_FOX),
    ('nsa_q', N_HEADS_NSA * HEAD_DIM),
    ('nsa_k_cmp', HEAD_DIM), ('nsa_v_cmp', HEAD_DIM),
    ('nsa_k_slc', HEAD_DIM), ('nsa_v_slc', HEAD_DIM),
    ('nsa_k_win', HEAD_DIM), ('nsa_v_win', HEAD_DIM),
    ('nsa_gate', 3 * N_HEADS_NSA),
    ('swa_q', N_HEADS_SWA * HEAD_DIM), ('swa_k', N_KV_SWA * HEAD_DIM), ('swa_v', N_KV_SWA * HEAD_DIM),
    ('dil_q', N_HEADS_DIL * HEAD_DIM), ('dil_k', N_HEADS_DIL * HEAD_DIM), ('dil_v', N_HEADS_DIL * HEAD_DIM),
)
D_IN = sum(w for _, w in IN_SPLITS)

kernel_name = 'hymba_style_fox_nsa_swa_dilated_macaron'


def rms_norm(x, g):
    xf = x.astype(jnp.float32)
    y = xf * lax.rsqrt(jnp.mean(xf * xf, axis=-1, keepdims=True) + RMS_EPS)
    return (y * g.astype(jnp.float32)).astype(x.dtype)


def swiglu(x, w_gate, w_up, w_down):
    return (jax.nn.silu(x @ w_gate) * (x @ w_up)) @ w_down


def split_columns(z):
    parts = {}
    off = 0
    for name, w in IN_SPLITS:
        parts[name] = z[..., off:off + w]
        off += w
    return parts


def to_heads(t, n):
    b, s, _ = t.shape
    return t.reshape(b, s, n, HEAD_DIM).transpose(0, 2, 1, 3)


def alibi_slopes():
    n = N_HEADS_SWA + N_HEADS_NSA + N_HEADS_DIL
    return jnp.asarray(2.0 ** (-8.0 * np.arange(1, n + 1) / n), jnp.float32)


def masked_softmax(s, mask):
    s = jnp.where(mask, s, -jnp.inf)
    m = jnp.max(s, axis=-1, keepdims=True)
    m = jnp.where(jnp.isfinite(m), m, 0.0)
    e = jnp.where(mask, jnp.exp(s - m), 0.0)
    return e / jnp.maximum(jnp.sum(e, axis=-1, keepdims=True), 1e-30)


def banded_attention(q, k, v, max_dist, slopes):
    b, h, l, hd = q.shape
    n_prev = -(-max_dist // BLOCK)
    nb = -(-l // BLOCK)
    pad = nb * BLOCK - l
    qb = jnp.pad(q, ((0, 0), (0, 0), (0, pad), (0, 0))).reshape(b, h, nb, BLOCK, hd)

    def windows(t):
        tp = jnp.pad(t, ((0, 0), (0, 0), (n_prev * BLOCK, pad), (0, 0))).reshape(b, h, nb + n_prev, BLOCK, hd)
        return jnp.concatenate([tp[:, :, j:j + nb] for j in range(n_prev + 1)], axis=3)

    kw, vw = windows(k), windows(v)
    width = (n_prev + 1) * BLOCK
    qpos = jnp.arange(nb)[:, None, None] * BLOCK + jnp.arange(BLOCK)[None, :, None]
    kpos = jnp.arange(nb)[:, None, None] * BLOCK + jnp.arange(width)[None, None, :] - n_prev * BLOCK
    dist = qpos - kpos
    mask = (dist >= 0) & (dist <= max_dist) & (kpos >= 0)
    s = jnp.einsum('bhnqd,bhnkd->bhnqk', qb, kw).astype(jnp.float32) * (hd ** -0.5)
    s = s - slopes.astype(jnp.float32)[:, None, None, None] * dist.astype(jnp.float32)
    s = jnp.where(mask, s, -jnp.inf)
    m = jnp.max(s, axis=-1, keepdims=True)
    e = jnp.exp(s - m)
    den = jnp.sum(e, axis=-1, keepdims=True)
    out = jnp.einsum('bhnqk,bhnkd->bhnqd', (e / den).astype(v.dtype), vw)
    lse = (m + jnp.log(den))[..., 0]
    out = out.reshape(b, h, nb * BLOCK, hd)[:, :, :l]
    lse = lse.reshape(b, h, nb * BLOCK)[:, :, :l]
    return out, lse


def forgetting_attention(q, k, v, log_f):
    b, h, s_len, hd = q.shape
    c = jnp.cumsum(log_f.astype(jnp.float32), axis=-1)
    nb = s_len // BLOCK
    qb = q.reshape(b, h, nb, BLOCK, hd).transpose(2, 0, 1, 3, 4)
    cb = c.reshape(b, h, nb, BLOCK).transpose(2, 0, 1, 3)
    kpos = jnp.arange(s_len)

    def one_block(args):
        i, q_i, c_i = args
        s = jnp.einsum('bhqd,bhkd->bhqk', q_i, k).astype(jnp.float32) * (hd ** -0.5)
        s = s + c_i[..., :, None] - c[..., None, :]
        qpos = i * BLOCK + jnp.arange(BLOCK)
        s = jnp.where(kpos[None, :] <= qpos[:, None], s, -jnp.inf)
        p = jax.nn.softmax(s, axis=-1)
        return jnp.einsum('bhqk,bhkd->bhqd', p.astype(v.dtype), v)

    out = lax.map(one_block, (jnp.arange(nb), qb, cb))
    return out.transpose(1, 2, 0, 3, 4).reshape(b, h, s_len, hd)


def nsa_attention(q, k_cmp, v_cmp, k_slc, v_slc, k_win, v_win, gate_logits, cmp_pe, cmp_w1, cmp_w2, slopes):
    b, h, s_len, hd = q.shape
    scale = hd ** -0.5
    t_pos = jnp.arange(s_len)
    sl = slopes.astype(jnp.float32)

    n_chunk = s_len // NSA_CMP_STRIDE
    n_cmp = n_chunk - 1

    def compress(t, idx):
        ch = t.reshape(b, n_chunk, NSA_CMP_STRIDE, hd)
        blocks = jnp.concatenate([ch[:, :-1], ch[:, 1:]], axis=2) + cmp_pe[idx]
        hid = jax.nn.silu(jnp.einsum('bnld,lde->bne', blocks, cmp_w1[idx]))
        return hid @ cmp_w2[idx]

    kc, vc = compress(k_cmp, 0), compress(v_cmp, 1)
    cmp_start = jnp.arange(n_cmp) * NSA_CMP_STRIDE
    cmp_dist = t_pos[:, None] - (cmp_start + NSA_CMP_LEN - 1)[None, :]
    s_cmp = jnp.einsum('bhtd,bnd->bhtn', q, kc).astype(jnp.float32) * scale
    s_cmp = s_cmp - sl[:, None, None] * cmp_dist.astype(jnp.float32)
    p_cmp = masked_softmax(s_cmp, cmp_dist >= 0)
    o_cmp = jnp.einsum('bhtn,bnd->bhtd', p_cmp.astype(vc.dtype), vc)

    n_sel = s_len // NSA_SEL_LEN
    sel_start = jnp.arange(n_sel) * NSA_SEL_LEN
    overlap = ((cmp_start[:, None] <= sel_start[None, :] + NSA_SEL_LEN - 1)
               & (cmp_start[:, None] + NSA_CMP_LEN - 1 >= sel_start[None, :])).astype(jnp.float32)
    imp = jnp.einsum('bhtn,nj->btj', p_cmp, overlap)
    cur = t_pos // NSA_SEL_LEN
    jj = jnp.arange(n_sel)
    causal_sel = sel_start[None, :] <= t_pos[:, None]
    forced = (jj[None, :] == 0) | (jj[None, :] == cur[:, None]) | (jj[None, :] == cur[:, None] - 1)
    score = jnp.where(causal_sel, jnp.where(forced, jnp.inf, imp), -jnp.inf)
    n_top = min(NSA_TOPN, n_sel)
    top_val, top_idx = lax.top_k(score, n_top)
    sel_ok = top_val > -jnp.inf

    ks_blocks = k_slc.reshape(b, n_sel, NSA_SEL_LEN, hd)
    vs_blocks = v_slc.reshape(b, n_sel, NSA_SEL_LEN, hd)
    nb = s_len // BLOCK
    qb = q.reshape(b, h, nb, BLOCK, hd).transpose(2, 0, 1, 3, 4)
    ib = top_idx.reshape(b, nb, BLOCK, n_top).transpose(1, 0, 2, 3)
    okb = sel_ok.reshape(b, nb, BLOCK, n_top).transpose(1, 0, 2, 3)
    gather = jax.vmap(lambda blocks, ix: blocks[ix])
    n_keys = n_top * NSA_SEL_LEN

    def one_block(args):
        i, q_i, idx_i, ok_i = args
        kg = gather(ks_blocks, idx_i).reshape(b, BLOCK, n_keys, hd)
        vg = gather(vs_blocks, idx_i).reshape(b, BLOCK, n_keys, hd)
        kpos = (idx_i[..., None] * NSA_SEL_LEN + jnp.arange(NSA_SEL_LEN)).reshape(b, BLOCK, n_keys)
        qpos = i * BLOCK + jnp.arange(BLOCK)
        dist = qpos[None, :, None] - kpos
        mask = (dist >= 0) & jnp.repeat(ok_i, NSA_SEL_LEN, axis=-1)
        s = jnp.einsum('bhqd,bqkd->bhqk', q_i, kg).astype(jnp.float32) * scale
        s = s - sl[None, :, None, None] * dist[:, None].astype(jnp.float32)
        p = masked_softmax(s, mask[:, None])
        return jnp.einsum('bhqk,bqkd->bhqd', p.astype(vg.dtype), vg)

    o_slc = lax.map(one_block, (jnp.arange(nb), qb, ib, okb))
    o_slc = o_slc.transpose(1, 2, 0, 3, 4).reshape(b, h, s_len, hd)

    kw = jnp.broadcast_to(k_win[:, None], (b, h, s_len, hd))
    vw = jnp.broadcast_to(v_win[:, None], (b, h, s_len, hd))
    o_win, _ = banded_attention(q, kw, vw, NSA_WINDOW - 1, slopes)

    g = jax.nn.sigmoid(gate_logits).reshape(b, s_len, h, 3).transpose(0, 2, 1, 3).astype(q.dtype)
    return g[..., 0:1] * o_cmp + g[..., 1:2] * o_slc + g[..., 2:3] * o_win


def sink_window_attention(q, k, v, sinks, slopes):
    rep = q.shape[1] // k.shape[1]
    k = jnp.repeat(k, rep, axis=1)
    v = jnp.repeat(v, rep, axis=1)
    out, lse = banded_attention(q, k, v, SWA_WINDOW - 1, slopes)
    keep = jax.nn.sigmoid(lse - sinks.astype(jnp.float32)[None, :, None])
    return out * keep[..., None].astype(out.dtype)


def dilated_attention(q, k, v, slopes):
    b, h, s_len, hd = q.shape
    outs, lses = [], []
    for window, d in DIL_PAIRS:
        def strided(t):
            return t.reshape(b, h, s_len // d, d, hd).transpose(0, 1, 3, 2, 4).reshape(b, h * d, s_len // d, hd)
        o, l = banded_attention(strided(q), strided(k), strided(v), window // d, jnp.repeat(slopes, d) * d)
        outs.append(o.reshape(b, h, d, s_len // d, hd).transpose(0, 1, 3, 2, 4).reshape(b, h, s_len, hd))
        lses.append(l.reshape(b, h, d, s_len // d).transpose(0, 1, 3, 2).reshape(b, h, s_len))
    w = jax.nn.softmax(jnp.stack(lses, axis=0), axis=0)
    return jnp.einsum('gbhs,gbhsd->bhsd', w.astype(q.dtype), jnp.stack(outs, axis=0))


def hybrid_mixer(hn, w_in, fox_b_f, nsa_cmp_pe, nsa_cmp_w1, nsa_cmp_w2, swa_sinks, w_out):
    b, s_len, _ = hn.shape
    z = split_columns(hn @ w_in)
    slopes = alibi_slopes()
    sl_swa = slopes[:N_HEADS_SWA]
    sl_nsa = slopes[N_HEADS_SWA:N_HEADS_SWA + N_HEADS_NSA]
    sl_dil = slopes[N_HEADS_SWA + N_HEADS_NSA:]

    log_f = jax.nn.log_sigmoid((z['fox_f'] + fox_b_f).astype(jnp.float32)).transpose(0, 2, 1)
    o_a = forgetting_attention(to_heads(z['fox_q'], N_HEADS_FOX), to_heads(z['fox_k'], N_HEADS_FOX),
                               to_heads(z['fox_v'], N_HEADS_FOX), log_f)
    o_b = nsa_attention(to_heads(z['nsa_q'], N_HEADS_NSA), z['nsa_k_cmp'], z['nsa_v_cmp'],
                        z['nsa_k_slc'], z['nsa_v_slc'], z['nsa_k_win'], z['nsa_v_win'], z['nsa_gate'],
                        nsa_cmp_pe, nsa_cmp_w1, nsa_cmp_w2, sl_nsa)
    o_c = sink_window_attention(to_heads(z['swa_q'], N_HEADS_SWA), to_heads(z['swa_k'], N_KV_SWA),
                                to_heads(z['swa_v'], N_KV_SWA), swa_sinks, sl_swa)
    o_d = dilated_attention(to_heads(z['dil_q'], N_HEADS_DIL), to_heads(z['dil_k'], N_HEADS_DIL),
                            to_heads(z['dil_v'], N_HEADS_DIL), sl_dil)
    o = jnp.concatenate([o_a, o_b.astype(o_a.dtype), o_c.astype(o_a.dtype), o_d.astype(o_a.dtype)], axis=1)
    o = o.transpose(0, 2, 1, 3).reshape(b, s_len, D_MIX).astype(hn.dtype)
    return o @ w_out


def setup_inputs(seed: int = 0) -> dict:
    key = jax.random.key(seed)
    ks = jax.random.split(key, 20)
    f32 = jnp.float32
    L, D, F = DEPTH, D_MODEL, D_FF

    def nrm(k, shape, scale):
        return jax.random.normal(k, shape, f32) * scale

    def gain(k, shape):
        return 1.0 + 0.02 * jax.random.normal(k, shape, f32)

    return {
        'x': nrm(ks[0], (BATCH, SEQ, D), 1.0),
        'norm_ffn1': gain(ks[1], (L, D)),
        'ffn1_w_gate': nrm(ks[2], (L, D, F), D ** -0.5),
        'ffn1_w_up': nrm(ks[3], (L, D, F), D ** -0.5),
        'ffn1_w_down': nrm(ks[4], (L, F, D), F ** -0.5),
        'norm_mix': gain(ks[5], (L, D)),
        'w_in': nrm(ks[6], (L, D, D_IN), D ** -0.5),
        'fox_b_f': 3.0 + 0.1 * jax.random.normal(ks[7], (L, N_HEADS_FOX), f32),
        'nsa_cmp_pe': nrm(ks[8], (L, 2, NSA_CMP_LEN, HEAD_DIM), 0.02),
        'nsa_cmp_w1': nrm(ks[9], (L, 2, NSA_CMP_LEN, HEAD_DIM, NSA_CMP_HIDDEN), (NSA_CMP_LEN * HEAD_DIM) ** -0.5),
        'nsa_cmp_w2': nrm(ks[10], (L, 2, NSA_CMP_HIDDEN, HEAD_DIM), NSA_CMP_HIDDEN ** -0.5),
        'swa_sinks': nrm(ks[11], (L, N_HEADS_SWA), 0.5),
        'w_out': nrm(ks[12], (L, D_MIX, D), D_MIX ** -0.5),
        'norm_ffn2': gain(ks[13], (L, D)),
        'ffn2_w_gate': nrm(ks[14], (L, D, F), D ** -0.5),
        'ffn2_w_up': nrm(ks[15], (L, D, F), D ** -0.5),
        'ffn2_w_down': nrm(ks[16], (L, F, D), F ** -0.5),
        'norm_final': gain(ks[17], (D,)),
    }


def reference(x, norm_ffn1, ffn1_w_gate, ffn1_w_up, ffn1_w_down, norm_mix, w_in, fox_b_f,
              nsa_cmp_pe, nsa_cmp_w1, nsa_cmp_w2, swa_sinks, w_out,
              norm_ffn2, ffn2_w_gate, ffn2_w_up, ffn2_w_down, norm_final):
    for l in range(DEPTH):
        x = x + 0.5 * swiglu(rms_norm(x, norm_ffn1[l]), ffn1_w_gate[l], ffn1_w_up[l], ffn1_w_down[l])
        x = x + hybrid_mixer(rms_norm(x, norm_mix[l]), w_in[l], fox_b_f[l], nsa_cmp_pe[l],
                             nsa_cmp_w1[l], nsa_cmp_w2[l], swa_sinks[l], w_out[l])
        x = x + 0.5 * swiglu(rms_norm(x, norm_ffn2[l]), ffn2_w_gate[l], ffn2_w_up[l], ffn2_w_down[l])
    return rms_norm(x, norm_final)
```

```python
import contextlib
import numpy as np
import concourse.bass as bass
import concourse.mybir as mybir
from concourse.bass_utils import run_bass_kernel_spmd

F32 = mybir.dt.float32
BF16 = mybir.dt.bfloat16
AF = mybir.ActivationFunctionType
ALU = mybir.AluOpType

D = 1024
T = 4096
FF = 2816
NF = 22
NK = 8
TB = 512
NBLK = 8
L = 2
NT = 32
NEG = -30000.0
EPS = 1e-6
NFM = 21
NCOLS = NFM * 128 + 512 + 272

ENGS = ("pe", "act", "dve", "pool", "sp")
SAME_ENGINE_SYNC = True
NDMASEM = 12
NDMASEM_Q = {"pool": 8, "sp": 12}


class Tok:
    __slots__ = ("name", "w", "r", "rd")

    def __init__(self, name=""):
        self.name = name
        self.w = None
        self.r = {}
        self.rd = []


class Ins:
    __slots__ = ("eng", "fn", "deps", "dma", "sem", "val", "signal", "count", "prev_dma")

    def __init__(self, eng, fn, dma=False):
        self.eng = eng
        self.fn = fn
        self.deps = []
        self.dma = dma
        self.sem = None
        self.val = 0
        self.signal = False
        self.count = 0
        self.prev_dma = None


class Prog:
    def __init__(self, nc):
        self.nc = nc
        self.q = {e: [] for e in ENGS}
        self.ndma = {e: 0 for e in ENGS}
        self.dma_last = {}
        self.bar = {e: [] for e in ENGS}
        self.all_dma_since_bar = []

    def _deps(self, ins, reads, writes):
        deps = []
        seen = set()

        def add(d):
            if d is None or d is ins or id(d) in seen:
                return
            seen.add(id(d))
            deps.append(d)

        for t in reads:
            add(t.w)
        for t in writes:
            add(t.w)
            for r in t.r.values():
                add(r)
            for r in t.rd:
                add(r)
        for d in self.bar[ins.eng]:
            add(d)
        self.bar[ins.eng] = []
        ins.deps = deps
        for t in reads:
            if ins.dma:
                t.rd.append(ins)
            else:
                t.r[ins.eng] = ins
        for t in writes:
            t.w = ins
            t.r = {}
            t.rd = []

    def op(self, eng, fn, reads=(), writes=()):
        ins = Ins(eng, fn)
        self._deps(ins, reads, writes)
        self.q[eng].append(ins)
        return ins

    def dma(self, eng, fn, reads=(), writes=()):
        ins = Ins(eng, fn, dma=True)
        i = self.ndma[eng]
        self.ndma[eng] += 1
        nq = NDMASEM_Q.get(eng, NDMASEM)
        ins.sem = (eng, i % nq)
        ins.val = 16 * (i // nq + 1)
        ins.prev_dma = self.dma_last.get(ins.sem)
        self.dma_last[ins.sem] = ins
        self._deps(ins, reads, writes)
        self.q[eng].append(ins)
        self.all_dma_since_bar.append(ins)
        return ins

    def barrier(self):
        deps = []
        for e in ENGS:
            for ins in reversed(self.q[e]):
                if not ins.dma and ins.fn is not None:
                    deps.append(ins)
                    break
        deps += self.all_dma_since_bar
        self.all_dma_since_bar = []
        for e in ENGS:
            self.bar[e] = self.bar[e] + deps

    def finish(self):
        self.barrier()
        for e in ENGS:
            self.op(e, None)

    def emit(self):
        nc = self.nc
        for e in ENGS:
            for ins in self.q[e]:
                for d in ins.deps:
                    if not d.dma:
                        if d.eng == ins.eng and (d.eng == "pe" or not SAME_ENGINE_SYNC):
                            continue
                        d.signal = True
        for e in ENGS:
            c = 0
            for ins in self.q[e]:
                if ins.signal:
                    c += 1
                ins.count = c
        with contextlib.ExitStack() as st:
            S = {e: st.enter_context(nc.semaphore("s_" + e)) for e in ENGS}
            Dm = {}
            for e in ENGS:
                for k in range(min(NDMASEM, self.ndma[e])):
                    Dm[(e, k)] = st.enter_context(nc.semaphore("d_%s%d" % (e, k)))
            block = st.enter_context(nc.Block())
            engobj = {"pe": "tensor", "act": "scalar", "dve": "vector", "pool": "gpsimd", "sp": "sync"}

            def run(ename, eng):
                waited = {}
                for ins in self.q[ename]:
                    deps = list(ins.deps)
                    if ins.dma and ins.prev_dma is not None:
                        deps.append(ins.prev_dma)
                    for d in deps:
                        if d.dma:
                            key = d.sem
                            val = d.val
                            sem = Dm[d.sem]
                        else:
                            if d.eng == ename and (ename == "pe" or not SAME_ENGINE_SYNC):
                                continue
                            key = d.eng
                            val = d.count
                            sem = S[d.eng]
                        if waited.get(key, 0) >= val:
                            continue
                        waited[key] = val
                        eng.wait_ge(sem, val)
                    if ins.fn is None:
                        continue
                    r = ins.fn(eng)
                    if ins.dma:
                        r.then_inc(Dm[ins.sem], 16)
                    elif ins.signal:
                        r.then_inc(S[ename], 1)

            for ename in ENGS:
                if not self.q[ename]:
                    continue
                getattr(block, engobj[ename])(lambda eng, ename=ename: run(ename, eng))


class Buf:
    def __init__(self, t, name=""):
        self.t = t
        self.tok = Tok(name)

    def __getitem__(self, k):
        return self.t[k]


class Arena:
    def __init__(self, nc, lo=16512, hi=229344):
        self.nc = nc
        self.lo = lo
        self.hi = hi
        self.cur = lo
        self.n = 0

    def alloc(self, shape, dtype, name="b"):
        esz = 4 if dtype == F32 else 2
        per = 1
        for s in shape[1:]:
            per *= s
        nbytes = per * esz
        off = (self.cur + 63) // 64 * 64
        assert off + nbytes <= self.hi, ("SBUF overflow", name, off, nbytes)
        self.cur = off + nbytes
        self.n += 1
        t = self.nc.alloc_sbuf_tensor_at("%s_%d" % (name, self.n), list(shape), dtype, offset=off)
        return Buf(t, name)

    def mark(self):
        return self.cur

    def reset(self, m):
        self.cur = m


def alibi_slopes():
    n = 12
    return (2.0 ** (-8.0 * np.arange(1, n + 1) / n)).astype(np.float64)


def make_consts():
    import ml_dtypes

    bf = ml_dtypes.bfloat16
    c = {}
    sl = alibi_slopes()
    sl_swa, sl_nsa, sl_dil = sl[0:4], sl[4:8], sl[8:12]
    p = np.arange(128)
    c["ident_bf"] = np.eye(128, dtype=np.float32).astype(bf)
    c["ones_bf"] = np.ones((128, 128), np.float32).astype(bf)
    c["ident_f"] = np.eye(128, dtype=np.float32)
    c["ones_f"] = np.ones((128, 128), np.float32)
    c["tri_f"] = (p[:, None] <= p[None, :]).astype(np.float32)
    tq = np.arange(512)
    m = np.zeros((128, 4, 512), np.float32)
    for r in range(4):
        m[:, r, :] = np.where(128 * r + p[:, None] <= tq[None, :], 0.0, NEG)
    c["m_causal"] = m.astype(bf)
    m = np.zeros((128, 8, 512), np.float32)
    for ri, r in enumerate(range(-4, 4)):
        dist = tq[None, :] - (128 * r + p[:, None])
        m[:, ri, :] = np.where((dist >= 0) & (dist <= 511), 0.0, NEG)
    c["m_win"] = m.astype(bf)
    b = np.zeros((128, 4, 8), np.float32)
    for h in range(4):
        for ri, r in enumerate(range(-4, 4)):
            b[:, h, ri] = sl_nsa[h] * (128 * r + p - 255)
    c["b_win"] = b
    b = np.zeros((128, 4, 32), np.float32)
    for h in range(4):
        for ri, r in enumerate(range(-28, 4)):
            b[:, h, ri] = np.maximum(sl_nsa[h] * (128 * r + p - 255), -3.0e4)
    c["b_slc"] = b
    e = np.zeros((128, 32, 128), np.float32)
    for j in range(32):
        for pp in range(128):
            e[2 * j + pp // 64, j, pp] = 1.0
    c["expand"] = e.astype(bf)
    tq1 = np.arange(128)
    m = np.zeros((128, 2, 128), np.float32)
    for ri, r in enumerate((-1, 0)):
        dist = tq1[None, :] - (128 * r + p[:, None])
        m[:, ri, :] = np.where((dist >= 0) & (dist <= 127), 0.0, NEG)
    c["m_swa"] = m.astype(bf)
    b = np.zeros((128, 4, 2), np.float32)
    f = np.zeros((64, 4, 512), np.float32)
    for h in range(4):
        for ri, r in enumerate((-1, 0)):
            b[:, h, ri] = sl_swa[h] * (128 * r + p - 63)
        f[:, h, :] = np.tile(np.exp(sl_swa[h] * (tq1 - 63)), 4)[None, :]
    c["b_swa"] = b
    c["f_swa"] = f
    w = np.zeros((128, 20, 512), np.float32)
    for ri, r in enumerate(range(-16, 4)):
        dist = tq[None, :] - (128 * r + p[:, None])
        mult = ((dist >= 0) & (dist <= 128)).astype(np.float32)
        mult += ((dist >= 0) & (dist % 4 == 0) & (dist // 4 <= 128)).astype(np.float32)
        mult += ((dist >= 0) & (dist % 16 == 0) & (dist // 16 <= 128)).astype(np.float32)
        w[:, ri, :] = mult
    c["w_dil"] = w.astype(bf)
    b = np.zeros((128, 4, 20), np.float32)
    for h in range(4):
        for ri, r in enumerate(range(-16, 4)):
            b[:, h, ri] = sl_dil[h] * (128 * r + p - 255)
    c["b_dil"] = b
    m = np.zeros((128, 17, 128), np.float32)
    for di in range(17):
        m[:, di, :] = np.where(tq1[None, :] + 128 * di >= 16 * p[:, None] + 31, 0.0, NEG)
    c["m_cmp"] = m.astype(bf)
    b = np.zeros((128, 4, 48), np.float32)
    for h in range(4):
        for di, dd in enumerate(range(-16, 32)):
            b[:, h, di] = np.maximum(sl_nsa[h] * (16 * p - 32 - 128 * dd), -3.0e4)
    c["b_cmp"] = b
    r = np.zeros((128, 2, 130), np.float32)
    for ch in range(2):
        for pp in range(128):
            n = 128 * ch + pp
            if n >= 255:
                continue
            for j in range(64):
                if 16 * n <= 64 * j + 63 and 16 * n + 31 >= 64 * j:
                    r[pp, ch, j] = 1.0
            r[pp, ch, 64] = 1.0
    c["r_cmp"] = r.astype(bf)
    A = np.zeros((128, 32, 64), np.float32)
    B = np.zeros((128, 32, 64), np.float32)
    Cz = np.zeros((128, 32, 64), np.float32)
    for tt in range(32):
        for pp in range(128):
            t = 128 * tt + pp
            cur = t // 64
            for j in range(64):
                causal = 64 * j <= t
                if not causal:
                    B[pp, tt, j] = -1.0 - 0.01 * j
                    continue
                Cz[pp, tt, j] = 1.0
                if j == 0:
                    B[pp, tt, j] = 12.0
                elif j == cur:
                    B[pp, tt, j] = 11.0
                elif j == cur - 1:
                    B[pp, tt, j] = 10.0
                else:
                    A[pp, tt, j] = 1.0
    c["sel_a"] = A
    c["sel_b"] = B
    c["sel_c"] = Cz
    return c


CONST_SPECS = None


def const_specs():
    global CONST_SPECS
    if CONST_SPECS is None:
        c = make_consts()
        CONST_SPECS = c
    return CONST_SPECS


def win_col_layout():
    off = {}
    o = 0
    splits = (("fox_q", 256), ("fox_k", 256), ("fox_v", 256), ("fox_f", 4), ("nsa_q", 256),
              ("nsa_k_cmp", 64), ("nsa_v_cmp", 64), ("nsa_k_slc", 64), ("nsa_v_slc", 64),
              ("nsa_k_win", 64), ("nsa_v_win", 64), ("nsa_gate", 12), ("swa_q", 256),
              ("swa_k", 128), ("swa_v", 128), ("dil_q", 256), ("dil_k", 256), ("dil_v", 256))
    for n, w in splits:
        off[n] = o
        o += w
    assert o == 2704

    def rng(n, a, b):
        return list(range(off[n] + a, off[n] + b))

    cols = []
    cols += rng("fox_q", 0, 256) + rng("fox_k", 0, 256)
    cols += rng("nsa_q", 0, 256)
    cols += rng("nsa_k_cmp", 0, 64) * 2 + rng("nsa_v_cmp", 0, 64) * 2
    cols += rng("nsa_k_slc", 0, 64) * 2 + rng("nsa_k_win", 0, 64) * 2
    cols += rng("swa_q", 0, 256) + rng("swa_k", 0, 128)
    cols += rng("dil_q", 0, 256) + rng("dil_k", 0, 256)
    g = off["nsa_gate"]
    for c_ in (1, 2):
        for h in range(4):
            cols += [g + 3 * h + c_] * 64
    assert len(cols) == NFM * 128
    cols += rng("fox_v", 0, 256) + rng("nsa_v_slc", 0, 64) + rng("nsa_v_win", 0, 64) + rng("swa_v", 0, 128)
    cols += rng("dil_v", 0, 256) + rng("fox_f", 0, 4) + rng("nsa_gate", 0, 12)
    assert len(cols) == NCOLS
    return np.asarray(cols, np.int64)


class Builder:
    def __init__(self, cfg=None):
        self.cfg = cfg or {}
        nc = bass.Bass("TRN2", target_bir_lowering=False)
        self.nc = nc
        self.P = Prog(nc)
        self.A = Arena(nc)
        self.inputs = {}
        self.ps = [Buf(nc.alloc_psum_tensor("ps%d" % i, [128, 512], F32), "ps%d" % i) for i in range(7)]
        self.psb = Buf(nc.alloc_psum_tensor("psb", [128, 1024], BF16), "psb")
        self.dbg = {}

    def din(self, name, shape, dtype=F32):
        ap = self.nc.dram_tensor(name, list(shape), dtype, kind="ExternalInput").ap()
        self.inputs[name] = ap
        return ap

    def dscr(self, name, shape, dtype):
        kind = "ExternalOutput" if name in self.cfg.get("dump", ()) else "Internal"
        ap = self.nc.dram_tensor(name, list(shape), dtype, kind=kind).ap()
        return ap

    def load(self, buf, src, eng="sp", view=None):
        dst = buf.t[:] if view is None else view
        return self.P.dma(eng, lambda e: e.dma_start(out=dst, in_=src), writes=[buf.tok])

    def declare(self):
        nc = self.nc
        self.xT = self.din("xT", [D, T])
        self.yT = nc.dram_tensor("yT", [D, T], F32, kind="ExternalOutput").ap()
        self.norms = self.din("norms", [L * 3 + 1, 128, NK])
        self.w_ffn = {}
        for nm in ("ffn1_w_gate", "ffn1_w_up", "ffn2_w_gate", "ffn2_w_up"):
            self.w_ffn[nm] = self.din(nm, [L, D, FF])
        for nm in ("ffn1_w_down", "ffn2_w_down"):
            self.w_ffn[nm] = self.din(nm, [L, FF, D])
        self.w_in = self.din("w_in_ext", [L, D, NCOLS])
        self.w_out = self.din("w_out", [L, D, D])
        self.fox_b = self.din("fox_b_rep", [L, 128, 128])
        self.cmp_pe = self.din("cmp_pe_l", [L, 2, 128, 16])
        self.cmp_w1 = self.din("nsa_cmp_w1", [L, 2, 2048, 128])
        self.cmp_w2 = self.din("cmp_w2_dup", [L, 2, 128, 128])
        self.sinks = self.din("sinks_rep", [L, 64, 4])
        cs = const_specs()
        self.cin = {}
        for k, v in cs.items():
            self.cin[k] = self.din("c_" + k, v.shape, BF16 if v.dtype != np.float32 else F32)
        self.XT = self.dscr("XT", [D, T], F32)
        self.ZT = self.dscr("ZT", [17 * 128, T + 32], BF16)
        self.GT = self.dscr("GT", [4 * 128, T], F32)
        self.ZVA = self.dscr("ZVA", [T, 512], BF16)
        self.ZVB = self.dscr("ZVB", [T, 256], BF16)
        self.ZF = self.dscr("ZF", [T, 16], F32)
        self.OT = self.dscr("OT", [D, T], BF16)
        self.OCT = self.dscr("OCT", [256, T], F32)
        self.PART = self.dscr("PART", [256, T], F32)
        self.HT = self.dscr("HT", [D, T], BF16)
        self.WINB = self.dscr("WINB", [L, D, NCOLS], BF16)
        self.WOUTB = self.dscr("WOUTB", [L, D, D], BF16)
        self.tok_winb = [Tok() for _ in range(L)]
        self.tok_woutb = [Tok() for _ in range(L)]
        self.precast_done = False

    def setup_consts(self):
        A = self.A
        self.ident_bf = A.alloc([128, 128], BF16, "ident_bf")
        self.ones_bf = A.alloc([128, 128], BF16, "ones_bf")
        self.load(self.ident_bf, self.cin["ident_bf"])
        self.load(self.ones_bf, self.cin["ones_bf"])
        self.eps_t = A.alloc([128, 1], F32, "eps_t")
        self.P.op("dve", lambda e: e.memset(self.eps_t[:, :], EPS), writes=[self.eps_t.tok])
        self.gains = A.alloc([128, L * 3 + 1, NK], F32, "gains")
        self.load(self.gains, self.norms.rearrange("g p k -> p g k"))
        zpad = A.alloc([128, 17, 32], BF16, "zpad")
        ZTv0 = self.ZT.rearrange("(c p) t -> p c t", p=128)
        self.P.op("pool", lambda e: e.memset(zpad[:, :, :], 0.0), writes=[zpad.tok])
        self.P.dma("sp", lambda e: e.dma_start(out=ZTv0[:, :, T:T + 32], in_=zpad[:, :, :]), reads=[zpad.tok])
        self.WA = self.alloc_half("wa")
        self.mark_wb = A.mark()

    def rms_block(self, xb, sq, hT, rstd, gi, out_f32=None, tmp=None, part="ab"):
        P = self.P
        ps = self.ps[6]
        if "a" in part:
            for kc in range(NK):
                P.op("act", lambda e, kc=kc: e.activation(out=sq[:, kc, :], in_=xb[:, kc, :], func=AF.Square),
                     reads=[xb.tok], writes=[sq.tok])
        if "b" not in part:
            return
        for kc in range(NK):
            P.op("pe", lambda e, kc=kc: e.matmul(ps[:, :], lhsT=self.ones_bf[:, :], rhs=sq[:, kc, :],
                                                  start=(kc == 0), stop=(kc == NK - 1)),
                 reads=[sq.tok, self.ones_bf.tok], writes=[ps.tok])
        P.op("act", lambda e: e.activation(out=rstd[:, :], in_=ps[:, :], func=AF.Sqrt, bias=self.eps_t[:, 0:1],
                                           scale=1.0 / D), reads=[ps.tok, self.eps_t.tok], writes=[rstd.tok])
        P.op("dve", lambda e: e.reciprocal(out=rstd[:, :], in_=rstd[:, :]), reads=[rstd.tok], writes=[rstd.tok])
        if out_f32 is None and tmp is not None:
            for kc in range(NK):
                tp_ = tmp[kc % 2]
                P.op("act", lambda e, kc=kc, tp_=tp_: e.activation(out=tp_[:, :], in_=xb[:, kc, :], func=AF.Copy,
                                                                   scale=self.gains[:, gi, kc:kc + 1]),
                     reads=[xb.tok, self.gains.tok], writes=[tp_.tok])
                P.op("pool", lambda e, kc=kc, tp_=tp_: e.tensor_tensor(out=hT[:, kc, :], in0=tp_[:, :], in1=rstd[:, :],
                                                                       op=ALU.mult),
                     reads=[tp_.tok, rstd.tok], writes=[hT.tok])
            return
        dst = hT if out_f32 is None else out_f32
        for kc in range(NK):
            P.op("dve",
                 lambda e, kc=kc: e.scalar_tensor_tensor(out=dst[:, kc, :], in0=xb[:, kc, :],
                                                         scalar=self.gains[:, gi, kc:kc + 1], in1=rstd[:, :],
                                                         op0=ALU.mult, op1=ALU.mult),
                 reads=[xb.tok, rstd.tok, self.gains.tok], writes=[dst.tok])

    def alloc_half(self, name):
        A = self.A
        return {"g": A.alloc([128, NK, FF // 2], BF16, name + "g"), "u": A.alloc([128, NK, FF // 2], BF16, name + "u"),
                "d": A.alloc([128, NF // 2, D], BF16, name + "d")}

    def load_ffn_half(self, W, l, which, hf, defer=False):
        P = self.P
        wg_d = self.w_ffn["ffn%d_w_gate" % which][l].rearrange("(c p) f -> p c f", p=128)
        wu_d = self.w_ffn["ffn%d_w_up" % which][l].rearrange("(c p) f -> p c f", p=128)
        wd_d = self.w_ffn["ffn%d_w_down" % which][l].rearrange("(c p) n -> p c n", p=128)
        fsl = slice(hf * (FF // 2), (hf + 1) * (FF // 2))
        th = []
        for kc in range(NK):
            for (key, w_d) in (("g", wg_d), ("u", wu_d)):
                th.append(lambda key=key, w_d=w_d, kc=kc: P.dma(
                    "pool", lambda e: e.dma_start(out=W[key][:, kc, :], in_=w_d[:, kc, fsl]), writes=[W[key].tok]))
        for fc in range(NF // 2):
            th.append(lambda fc=fc: P.dma(
                "pool", lambda e: e.dma_start(out=W["d"][:, fc, :], in_=wd_d[:, hf * (NF // 2) + fc, :]),
                writes=[W["d"].tok]))
        if defer:
            return th
        for f in th:
            f()
        return []

    def ffn_phase(self, l, which, src, dst, nxt):
        P, A = self.P, self.A
        A.reset(self.mark_wb)
        mk = A.mark()
        WA = self.WA
        WB = self.alloc_half("wb")
        gi = l * 3 + (0 if which == 1 else 2)
        wq = self.load_ffn_half(WB, l, which, 1, defer=True)
        if not self.precast_done:
            self.precast_done = True
            for ll in range(self.cfg.get("layers", L)):
                for kc in range(NK):
                    rs = slice(kc * 128, (kc + 1) * 128)
                    wq.append(lambda ll=ll, rs=rs: P.dma("pool", lambda e: e.dma_start(
                        out=self.WINB[ll, rs, :], in_=self.w_in[ll, rs, :]), writes=[self.tok_winb[ll]]))
                for kc in range(NK):
                    rs = slice(kc * 128, (kc + 1) * 128)
                    wq.append(lambda ll=ll, rs=rs: P.dma("pool", lambda e: e.dma_start(
                        out=self.WOUTB[ll, rs, :], in_=self.w_out[ll, rs, :]), writes=[self.tok_woutb[ll]]))

        def issue_w(n):
            for _ in range(n):
                if wq:
                    wq.pop(0)()
        xb = A.alloc([128, NK, TB], F32, "xb")
        sq = A.alloc([128, NK, TB], BF16, "sq")
        hTs = [A.alloc([128, NK, TB], BF16, "hT%d" % i) for i in range(2)]
        aT = A.alloc([128, NF // 2, TB], BF16, "aT")
        rstd = A.alloc([128, TB], F32, "rstd")
        sg = [A.alloc([128, TB], BF16, "sg%d" % i) for i in range(2)]
        stg = [A.alloc([128, TB], F32, "stg%d" % i) for i in range(4)]
        ntmp = [A.alloc([128, TB], F32, "ntmp%d" % i) for i in range(2)]
        nblk = self.cfg.get("nblk", NBLK)
        HTv = self.HT.rearrange("(c p) t -> p c t", p=128)
        nst = [0]
        tokHT = [Tok() for _ in range(nblk)]
        tokX = [[Tok() for _ in range(NK)] for _ in range(nblk)]
        for ps_ in range(2):
            W = WA if ps_ == 0 else WB
            rsrc = src if ps_ == 0 else dst
            srcv = rsrc.rearrange("(c p) t -> p c t", p=128)
            if ps_ == 1:
                issue_w(len(wq))
                if nxt is not None:
                    wq.extend(self.load_ffn_half(WA, nxt[0], nxt[1], 0, defer=True))

            def prep(blk):
                cs = slice(blk * TB, (blk + 1) * TB)
                hT = hTs[blk % 2]
                if ps_ == 0:
                    P.dma("sp" if blk == 0 else "act", lambda e, srcv=srcv, cs=cs: e.dma_start(
                        out=xb[:, :, :], in_=srcv[:, :, cs]), writes=[xb.tok])
                else:
                    P.dma("sp", lambda e, hT=hT, cs=cs: e.dma_start(out=hT[:, :, :], in_=HTv[:, :, cs]),
                          reads=[tokHT[blk]], writes=[hT.tok])

            def norm(blk, part="ab"):
                cs = slice(blk * TB, (blk + 1) * TB)
                hT = hTs[blk % 2]
                if ps_ == 0:
                    self.rms_block(xb, sq, hT, rstd, gi, tmp=ntmp, part=part)
                    if "b" not in part:
                        return
                    htq.append(lambda hT=hT, cs=cs, blk=blk: P.dma(
                        "sp", lambda e: e.dma_start(out=HTv[:, :, cs], in_=hT[:, :, :]),
                        reads=[hT.tok], writes=[tokHT[blk]]))

            htq = []
            prep(0)
            norm(0)
            if ps_ == 0 and nblk > 1:
                prep(1)
            for blk in range(nblk):
                cs = slice(blk * TB, (blk + 1) * TB)
                hT = hTs[blk % 2]
                while htq:
                    htq.pop(0)()
                if ps_ == 1 and blk + 1 < nblk:
                    prep(blk + 1)
                issue_w(8 if len(wq) > 32 else 5)
                for f in range(NF // 2):
                    pg, pu = self.ps[f % 2], self.ps[2 + f % 2]
                    fs = slice(f * 128, (f + 1) * 128)
                    for (pp, key) in ((pg, "g"), (pu, "u")):
                        for kc in range(NK):
                            P.op("pe", lambda e, pp=pp, key=key, kc=kc, fs=fs, hT=hT, W=W: e.matmul(
                                pp[:, :], lhsT=W[key][:, kc, fs], rhs=hT[:, kc, :], start=(kc == 0),
                                stop=(kc == NK - 1)), reads=[W[key].tok, hT.tok], writes=[pp.tok])
                    s_ = sg[f % 2]
                    P.op("act", lambda e, s_=s_, pg=pg: e.activation(out=s_[:, :], in_=pg[:, :], func=AF.Silu),
                         reads=[pg.tok], writes=[s_.tok])
                    P.op("dve", lambda e, s_=s_, pu=pu, f=f: e.tensor_tensor(out=aT[:, f, :], in0=pu[:, :],
                                                                             in1=s_[:, :], op=ALU.mult),
                         reads=[pu.tok, s_.tok], writes=[aT.tok])
                def ldres(n, cs=cs, rsrc=rsrc, blk=blk):
                    st = stg[(nst[0] + n) % 4]
                    P.dma("sp", lambda e, st=st, n=n, cs=cs, rsrc=rsrc: e.dma_start(
                        out=st[:, :], in_=rsrc[n * 128:(n + 1) * 128, cs]),
                        reads=([tokX[blk][n]] if ps_ == 1 else []), writes=[st.tok])

                ldres(0)
                ldres(1)
                for n in range(NK):
                    py = self.ps[4 + n % 2]
                    ns = slice(n * 128, (n + 1) * 128)
                    if n + 2 < NK:
                        ldres(n + 2)
                    if n == 0 and blk + 1 < nblk:
                        norm(blk + 1, part="a")
                    if n == 2 and blk + 1 < nblk:
                        norm(blk + 1, part="b")
                        if ps_ == 0 and blk + 2 < nblk:
                            prep(blk + 2)
                    for f in range(NF // 2):
                        P.op("pe", lambda e, py=py, f=f, ns=ns, W=W: e.matmul(
                            py[:, :], lhsT=W["d"][:, f, ns], rhs=aT[:, f, :], start=(f == 0),
                            stop=(f == NF // 2 - 1)), reads=[W["d"].tok, aT.tok], writes=[py.tok])
                    st = stg[(nst[0] + n) % 4]
                    P.op("dve", lambda e, py=py, st=st: e.scalar_tensor_tensor(
                        out=st[:, :], in0=py[:, :], scalar=0.5, in1=st[:, :], op0=ALU.mult, op1=ALU.add),
                        reads=[py.tok, st.tok], writes=[st.tok])
                    P.dma("sp", lambda e, st=st, ns=ns, cs=cs: e.dma_start(out=dst[ns, cs], in_=st[:, :]),
                          reads=[st.tok], writes=[tokX[blk][n]])
                nst[0] += NK
            issue_w(len(wq))
            if ps_ == 1:
                P.barrier()
        A.reset(mk)

    def final_phase(self, src):
        P, A = self.P, self.A
        A.reset(self.mark_wb)
        mk = A.mark()
        xbs = [A.alloc([128, NK, TB], F32, "fxb%d" % i) for i in range(2)]
        obs = [A.alloc([128, NK, TB], F32, "fob%d" % i) for i in range(2)]
        sq = A.alloc([128, NK, TB], BF16, "sq")
        rstd = A.alloc([128, TB], F32, "rstd")
        srcv = src.rearrange("(c p) t -> p c t", p=128)
        dstv = self.yT.rearrange("(c p) t -> p c t", p=128)
        nblk = self.cfg.get("nblk", NBLK)

        def ld(blk):
            cs = slice(blk * TB, (blk + 1) * TB)
            xb = xbs[blk % 2]
            P.dma("sp", lambda e: e.dma_start(out=xb[:, :, :], in_=srcv[:, :, cs]), writes=[xb.tok])

        ld(0)
        for blk in range(nblk):
            cs = slice(blk * TB, (blk + 1) * TB)
            if blk + 1 < nblk:
                ld(blk + 1)
            xb, ob = xbs[blk % 2], obs[blk % 2]
            self.rms_block(xb, sq, None, rstd, L * 3, out_f32=ob)
            P.dma("sp", lambda e, cs=cs, ob=ob: e.dma_start(out=dstv[:, :, cs], in_=ob[:, :, :]), reads=[ob.tok])
        P.barrier()
        A.reset(mk)

    def build(self):
        self.declare()
        self.setup_consts()
        cfg = self.cfg
        cur = self.xT
        nl = cfg.get("layers", L)
        ffns = []
        for l in range(nl):
            ffns.append((l, 1))
            if cfg.get("ffn2", True):
                ffns.append((l, 2))
        fi = 0
        self.load_ffn_half(self.WA, 0, 1, 0)
        for l in range(nl):
            nxt = ffns[fi + 1] if fi + 1 < len(ffns) else None
            self.ffn_phase(l, 1, cur, self.XT, nxt)
            fi += 1
            cur = self.XT
            if cfg.get("mixer", True):
                self.mixer(l, cur)
            if cfg.get("ffn2", True):
                nxt = ffns[fi + 1] if fi + 1 < len(ffns) else None
                self.ffn_phase(l, 2, cur, self.XT, nxt)
                fi += 1
        self.final_phase(cur)
        self.P.finish()
        self.P.emit()
        return self.nc

    def mixer(self, l, src):
        P, A = self.P, self.A
        A.reset(self.mark_wb)
        mk0 = A.mark()
        self.negselT = A.alloc([128, T], BF16, "negselT")
        self.proj_phase(l, src)
        stages = self.cfg.get("stages", ("fox", "nsa", "swa", "dil"))
        if "fox" in stages:
            self.fox_phase(l)
        if "nsa" in stages:
            self.nsa_cmp_phase(l)
            self.nsa_band_phase(l, "slc")
            self.nsa_band_phase(l, "win")
        if "swa" in stages:
            self.swa_phase(l)
        if "dil" in stages:
            self.dil_phase(l)
        self.outproj_phase(l, src)
        A.reset(mk0)

    def proj_phase(self, l, src):
        P, A = self.P, self.A
        mk = A.mark()
        win = A.alloc([128, NK, NCOLS], BF16, "win")
        wv = self.WINB[l].rearrange("(c p) n -> p c n", p=128)
        pieces = [(0, 896), (896, 1792), (1792, 2688), (2688, NCOLS)]
        wtok = [Tok() for _ in pieces]
        def load_win():
            for pi, (a, b) in enumerate(pieces):
                P.dma("act", lambda e, a=a, b=b: e.dma_start(out=win[:, :, a:b], in_=wv[:, :, a:b]),
                      reads=[self.tok_winb[l]], writes=[wtok[pi]])
        xb_ = A.alloc([128, NK, TB], F32, "xb")
        xbs = [xb_, xb_]
        sq = A.alloc([128, NK, TB], BF16, "sq")
        hTs = [A.alloc([128, NK, TB], BF16, "hT%d" % i) for i in range(2)]
        rstd = A.alloc([128, TB], F32, "rstd")
        zsl = [A.alloc([128, TB], BF16, "zsl%d" % i) for i in range(12)]
        gsl = [A.alloc([128, TB], F32, "gsl%d" % i) for i in range(2)]
        zva = [A.alloc([128, 4, 512], BF16, "zva%d" % i) for i in range(2)]
        zvb = [A.alloc([128, 4, 256], BF16, "zvb%d" % i) for i in range(2)]
        zf = [A.alloc([128, 4, 16], F32, "zf%d" % i) for i in range(2)]
        ntmp = [A.alloc([128, TB], F32, "ntmp%d" % i) for i in range(2)]
        srcv = src.rearrange("(c p) t -> p c t", p=128)
        ZTv = self.ZT.rearrange("(c p) t -> p c t", p=128)
        ZVAv = self.ZVA.rearrange("(s p) c -> p s c", p=128)
        ZVBv = self.ZVB.rearrange("(s p) c -> p s c", p=128)
        ZFv = self.ZF.rearrange("(s p) c -> p s c", p=128)
        gi = l * 3 + 1
        nblk = self.cfg.get("nblk", NBLK)

        def ldx(blk):
            cs = slice(blk * TB, (blk + 1) * TB)
            xb = xbs[blk % 2]
            P.dma("sp" if blk == 0 else "act", lambda e: e.dma_start(out=xb[:, :, :], in_=srcv[:, :, cs]),
                  writes=[xb.tok])

        ldx(0)
        load_win()
        self.rms_block(xbs[0], sq, hTs[0], rstd, gi, tmp=ntmp)
        if nblk > 1:
            ldx(1)
        nz = 0
        for blk in range(nblk):
            cs = slice(blk * TB, (blk + 1) * TB)
            hT = hTs[blk % 2]
            for c in range(NFM):
                pp = self.ps[c % 4]
                if c == 4 and blk + 1 < nblk:
                    self.rms_block(xbs[(blk + 1) % 2], sq, hTs[(blk + 1) % 2], rstd, gi, tmp=ntmp, part="a")
                if c == 8 and blk + 1 < nblk:
                    self.rms_block(xbs[(blk + 1) % 2], sq, hTs[(blk + 1) % 2], rstd, gi, tmp=ntmp, part="b")
                    if blk + 2 < nblk:
                        ldx(blk + 2)
                for kc in range(NK):
                    P.op("pe", lambda e, pp=pp, kc=kc, c=c, hT=hT: e.matmul(
                        pp[:, :], lhsT=win[:, kc, c * 128:(c + 1) * 128], rhs=hT[:, kc, :],
                        start=(kc == 0), stop=(kc == NK - 1)), reads=[wtok[c // 7], hT.tok], writes=[pp.tok])
                if c < 17:
                    zs = zsl[nz % 12]
                    nz += 1
                    if c % 2 == 0:
                        P.op("act", lambda e, pp=pp, zs=zs: e.activation(out=zs[:, :], in_=pp[:, :], func=AF.Copy),
                             reads=[pp.tok], writes=[zs.tok])
                    else:
                        P.op("dve", lambda e, pp=pp, zs=zs: e.tensor_copy(out=zs[:, :], in_=pp[:, :]),
                             reads=[pp.tok], writes=[zs.tok])
                    P.dma("sp", lambda e, zs=zs, c=c, cs=cs: e.dma_start(
                        out=self.ZT[c * 128:(c + 1) * 128, cs], in_=zs[:, :]), reads=[zs.tok])
                else:
                    gs_ = gsl[c % 2]
                    P.op("act", lambda e, pp=pp, gs_=gs_: e.activation(out=gs_[:, :], in_=pp[:, :],
                                                                       func=AF.Sigmoid), reads=[pp.tok], writes=[gs_.tok])
                    P.dma("sp", lambda e, gs_=gs_, c=c, cs=cs: e.dma_start(
                        out=self.GT[(c - 17) * 128:(c - 16) * 128, cs], in_=gs_[:, :]), reads=[gs_.tok])
            za, zb, zf_ = zva[blk % 2], zvb[blk % 2], zf[blk % 2]
            for sub in range(4):
                ss = slice(sub * 128, (sub + 1) * 128)
                pa = self.ps[4 + sub % 2]
                pb = self.ps[6]
                for kc in range(NK):
                    P.op("pe", lambda e, pa=pa, kc=kc, ss=ss, hT=hT: e.matmul(
                        pa[:, :], lhsT=hT[:, kc, ss], rhs=win[:, kc, 2688:3200], start=(kc == 0), stop=(kc == NK - 1)),
                        reads=[wtok[3], hT.tok], writes=[pa.tok])
                P.op("act", lambda e, pa=pa, sub=sub, za=za: e.activation(out=za[:, sub, :], in_=pa[:, :], func=AF.Copy),
                     reads=[pa.tok], writes=[za.tok])
                for kc in range(NK):
                    P.op("pe", lambda e, pb=pb, kc=kc, ss=ss, hT=hT: e.matmul(
                        pb[:, 0:272], lhsT=hT[:, kc, ss], rhs=win[:, kc, 3200:3472], start=(kc == 0),
                        stop=(kc == NK - 1)), reads=[wtok[3], hT.tok], writes=[pb.tok])
                P.op("dve", lambda e, pb=pb, sub=sub, zb=zb: e.tensor_copy(out=zb[:, sub, :], in_=pb[:, 0:256]),
                     reads=[pb.tok], writes=[zb.tok])
                P.op("dve", lambda e, pb=pb, sub=sub, zf_=zf_: e.tensor_copy(out=zf_[:, sub, :], in_=pb[:, 256:272]),
                     reads=[pb.tok], writes=[zf_.tok])
            bs = slice(blk * 4, (blk + 1) * 4)
            P.dma("sp", lambda e, bs=bs, za=za: e.dma_start(out=ZVAv[:, bs, :], in_=za[:, :, :]), reads=[za.tok])
            P.dma("sp", lambda e, bs=bs, zb=zb: e.dma_start(out=ZVBv[:, bs, :], in_=zb[:, :, :]), reads=[zb.tok])
            P.dma("sp", lambda e, bs=bs, zf_=zf_: e.dma_start(out=ZFv[:, bs, :], in_=zf_[:, :, :]), reads=[zf_.tok])
        P.barrier()
        A.reset(mk)

    def attn_env(self):
        A = self.A
        env = {}
        env["sb"] = [self.ps[0], self.ps[1], self.ps[4], self.ps[5], self.ps[6]]
        env["ob"] = [self.ps[2], self.ps[3]]
        env["pslot"] = [A.alloc([128, 512], BF16, "pslot%d" % i) for i in range(8)]
        env["dcp"] = [A.alloc([64, 512], F32, "dcp%d" % i) for i in range(2)]
        env["osb"] = [A.alloc([64, 512], BF16, "osb%d" % i) for i in range(2)]
        env["o32"] = [A.alloc([64, 512], F32, "o32%d" % i) for i in range(2)]
        env["sc"] = 0
        env["oc"] = 0
        env["pc"] = 0
        env["fc"] = 0
        env["pending"] = []
        env["la"] = 4
        return env

    def attn_flush(self, env, keep=0):
        while len(env["pending"]) > keep:
            f = env["pending"].pop(0)
            f()

    def attn_job(self, env, qT, NQ, nblocks, tiles_fn, fin_fn, ogroup=1):
        P = self.P
        ob = None
        for I in range(nblocks):
            tl = tiles_fn(I)
            if I % ogroup == 0:
                ob = env["ob"][env["oc"] % 2]
                env["oc"] += 1
            o0 = (I % ogroup) * NQ
            qs = slice(I * NQ, (I + 1) * NQ)
            ntl = len(tl)
            for idx, t in enumerate(tl):
                sb = env["sb"][env["sc"] % 5]
                env["sc"] += 1
                psl = env["pslot"][env["pc"] % 8]
                env["pc"] += 1
                adds = t.get("adds", [])
                c0, c1 = t.get("cr", (0, NQ))
                qcs = slice(I * NQ + c0, I * NQ + c1)
                P.op("pe", lambda e, sb=sb, t=t, qcs=qcs, adds=adds, c0=c0, c1=c1: e.matmul(
                    sb[:, c0:c1], lhsT=t["k"], rhs=qT[:, qcs], start=True, stop=(len(adds) == 0)),
                    reads=[qT.tok] + t["ktoks"], writes=[sb.tok])
                for ai, ad in enumerate(adds):
                    lh, rh, toks = ad[0], ad[1], ad[2]
                    a0, a1 = ad[3] if len(ad) > 3 else (c0, c1)
                    P.op("pe", lambda e, sb=sb, lh=lh, rh=rh, ai=ai, adds=adds, a0=a0, a1=a1: e.matmul(
                        sb[:, a0:a1], lhsT=lh, rhs=rh[:, a0:a1], start=False, stop=(ai == len(adds) - 1)),
                        reads=toks, writes=[sb.tok])
                P.op("act", lambda e, sb=sb, psl=psl, t=t, c0=c0, c1=c1: e.activation(
                    out=psl[:, c0:c1], in_=sb[:, c0:c1], func=AF.Exp, bias=t["bias"], scale=0.125),
                    reads=[sb.tok] + t["btoks"], writes=[psl.tok])
                if t.get("mult") is not None:
                    mp, mtok = t["mult"]
                    env["mc"] = env.get("mc", 0) + 1
                    P.op("pool" if env["mc"] % 2 == 0 else "dve", lambda e, psl=psl, mp=mp, c0=c0, c1=c1: e.tensor_tensor(
                        out=psl[:, c0:c1], in0=psl[:, c0:c1], in1=mp[:, c0:c1], op=ALU.mult),
                        reads=[psl.tok, mtok], writes=[psl.tok])

                def pv(ob=ob, t=t, psl=psl, idx=idx, ntl=ntl, I=I, o0=o0, c0=c0, c1=c1):
                    P.op("pe", lambda e: e.matmul(ob[:, o0 + c0:o0 + c1], lhsT=t["v"], rhs=psl[:, c0:c1],
                                                  start=(idx == 0), stop=(idx == ntl - 1)),
                         reads=[psl.tok] + t["vtoks"], writes=[ob.tok])
                    if idx == ntl - 1 and (I % ogroup == ogroup - 1):
                        fin_fn(I // ogroup, ob)

                env["pending"].append(pv)
                self.attn_flush(env, keep=env["la"])

    def fin_norm(self, env, ob, NQ, hook=None, out32=False):
        P = self.P
        k = env["fc"] % 2
        env["fc"] += 1
        dcp = env["dcp"][k]
        if env.get("act_recip"):
            if hook is not None:
                P.op("dve", lambda e: e.tensor_copy(out=dcp[:, 0:NQ], in_=ob[64:128, 0:NQ]), reads=[ob.tok],
                     writes=[dcp.tok])
                hook(dcp)
                P.op("act", lambda e: e.activation(out=dcp[:, 0:NQ], in_=dcp[:, 0:NQ], func=AF.Ln), reads=[dcp.tok],
                     writes=[dcp.tok])
            else:
                P.op("act", lambda e: e.activation(out=dcp[:, 0:NQ], in_=ob[64:128, 0:NQ], func=AF.Ln), reads=[ob.tok],
                     writes=[dcp.tok])
            P.op("act", lambda e: e.activation(out=dcp[:, 0:NQ], in_=dcp[:, 0:NQ], func=AF.Exp, scale=-1.0),
                 reads=[dcp.tok], writes=[dcp.tok])
        else:
            if env.get("act_copy"):
                P.op("act", lambda e: e.activation(out=dcp[:, 0:NQ], in_=ob[64:128, 0:NQ], func=AF.Copy),
                     reads=[ob.tok], writes=[dcp.tok])
            else:
                P.op("dve", lambda e: e.tensor_copy(out=dcp[:, 0:NQ], in_=ob[64:128, 0:NQ]), reads=[ob.tok],
                     writes=[dcp.tok])
            if hook is not None:
                hook(dcp)
            P.op("dve", lambda e: e.reciprocal(out=dcp[:, 0:NQ], in_=dcp[:, 0:NQ]), reads=[dcp.tok], writes=[dcp.tok])
        dst = env["o32"][k] if out32 else env["osb"][k]
        P.op("dve", lambda e: e.tensor_tensor(out=dst[:, 0:NQ], in0=ob[0:64, 0:NQ], in1=dcp[:, 0:NQ], op=ALU.mult),
             reads=[ob.tok, dcp.tok], writes=[dst.tok])
        return dst

    def head_bufs(self, n=2):
        A, P = self.A, self.P
        sets = []
        for i in range(n):
            s = {"q": A.alloc([128, T], BF16, "qT%d" % i),
                 "kt": A.alloc([128, T], BF16, "kpt%d" % i),
                 "kb": A.alloc([128, T], BF16, "kpb%d" % i),
                 "v": A.alloc([128, NT, 128], BF16, "vaug%d" % i)}
            P.op("pool", lambda e, s=s: e.memset(s["kt"][64:128, :], 0.0), writes=[s["kt"].tok])
            P.op("dve", lambda e, s=s: e.memset(s["kb"][0:64, :], 0.0), writes=[s["kb"].tok])
            P.op("pool" if i % 2 else "dve", lambda e, s=s: e.memset(s["v"][:, :, 64:128], 1.0), writes=[s["v"].tok])
            sets.append(s)
        return sets

    def load_q(self, buf, chunk):
        self.P.dma("sp", lambda e: e.dma_start(out=buf[:, :], in_=self.ZT[chunk * 128:(chunk + 1) * 128, 0:T]),
                   writes=[buf.tok])

    def load_k(self, buf, chunk, src_half, dst_half):
        r0 = chunk * 128 + src_half * 64
        self.P.dma("sp", lambda e: e.dma_start(out=buf[dst_half * 64:(dst_half + 1) * 64, :],
                                               in_=self.ZT[r0:r0 + 64, 0:T]), writes=[buf.tok])

    def load_v(self, buf, srcv, col):
        self.P.dma("sp", lambda e: e.dma_start(out=buf[:, :, 0:64], in_=srcv[:, :, col:col + 64]), writes=[buf.tok])

    def cload(self, name, shape, dtype):
        b = self.A.alloc(shape, dtype, name)
        self.load(b, self.cin[name])
        return b

    def store_ot(self, dst_tile, NQ, row0, I):
        self.P.dma("sp", lambda e: e.dma_start(out=self.OT[row0:row0 + 64, I * NQ:(I + 1) * NQ],
                                               in_=dst_tile[:, 0:NQ]), reads=[dst_tile.tok])

    def fox_phase(self, l):
        P, A = self.P, self.A
        mk = A.mark()
        env = self.attn_env()
        hb = self.head_bufs(2)
        mc = self.cload("m_causal", [128, 4, 512], BF16)
        tri = self.cload("tri_f", [128, 128], F32)
        onesf = self.cload("ones_f", [128, 128], F32)
        zfs = A.alloc([128, NT, 16], F32, "zfs")
        self.load(zfs, self.ZF.rearrange("(s p) c -> p s c", p=128))
        bF = A.alloc([128, NT, 4], F32, "bF")
        self.load(bF, self.fox_b[l].rearrange("p (s h) -> p s h", h=4))
        lf = A.alloc([128, NT, 4], F32, "lf")
        P.op("dve", lambda e: e.tensor_tensor(out=lf[:, :, :], in0=zfs[:, :, 0:4], in1=bF[:, :, :], op=ALU.add),
             reads=[zfs.tok, bF.tok], writes=[lf.tok])
        P.op("act", lambda e: e.activation(out=lf[:, :, :], in_=lf[:, :, :], func=AF.Sigmoid), reads=[lf.tok],
             writes=[lf.tok])
        P.op("act", lambda e: e.activation(out=lf[:, :, :], in_=lf[:, :, :], func=AF.Ln), reads=[lf.tok],
             writes=[lf.tok])
        pc, pt = self.ps[4], self.ps[5]
        lf2 = lf[:, :, :].rearrange("p s h -> p (s h)")
        P.op("pe", lambda e: e.matmul(pc[:, 0:128], lhsT=tri[:, :], rhs=lf2, start=True, stop=True),
             reads=[tri.tok, lf.tok], writes=[pc.tok])
        P.op("pe", lambda e: e.matmul(pt[:, 0:128], lhsT=onesf[:, :], rhs=lf2, start=True, stop=True),
             reads=[onesf.tok, lf.tok], writes=[pt.tok])
        offs = A.alloc([128, NT + 1, 4], F32, "offs")
        P.op("dve", lambda e: e.memset(offs[:, 0, :], 0.0), writes=[offs.tok])
        for j in range(1, NT + 1):
            P.op("dve", lambda e, j=j: e.tensor_tensor(out=offs[:, j, :], in0=offs[:, j - 1, :],
                                                       in1=pt[:, (j - 1) * 4:j * 4], op=ALU.add),
                 reads=[offs.tok, pt.tok], writes=[offs.tok])
        cf = A.alloc([128, NT, 4], F32, "cf")
        P.op("dve", lambda e: e.tensor_tensor(out=cf[:, :, :], in0=offs[:, 0:NT, :],
                                              in1=pc[:, 0:128].rearrange("p (s h) -> p s h", h=4), op=ALU.add),
             reads=[offs.tok, pc.tok], writes=[cf.tok])
        bias = A.alloc([128, NBLK, NT, 4], F32, "biasF")
        for I in range(NBLK):
            J = 4 * I + 4
            for h in range(4):
                P.op("dve", lambda e, I=I, J=J, h=h: e.tensor_scalar(
                    out=bias[:, I, 0:J, h], in0=cf[:, 0:J, h], scalar1=-1.0, scalar2=offs[:, J, h:h + 1],
                    op0=ALU.mult, op1=ALU.add), reads=[cf.tok, offs.tok], writes=[bias.tok])
        ZVAv = self.ZVA.rearrange("(s p) c -> p s c", p=128)
        for h in range(4):
            s = hb[h % 2]
            half = h % 2
            kp = s["kt"] if half == 0 else s["kb"]
            self.load_q(s["q"], 0 + h // 2)
            self.load_k(kp, 2 + h // 2, half, half)
            self.load_v(s["v"], ZVAv, h * 64)

            def tiles(I, s=s, kp=kp, h=h):
                tl = []
                for j in range(4 * I + 4):
                    t = {"k": kp[:, j * 128:(j + 1) * 128], "ktoks": [kp.tok], "v": s["v"][:, j, :],
                         "vtoks": [s["v"].tok], "bias": bias[:, I, j, h:h + 1], "btoks": [bias.tok]}
                    if j >= 4 * I:
                        t["adds"] = [(self.ident_bf[:, :], mc[:, j - 4 * I, :], [self.ident_bf.tok, mc.tok],
                                      (128 * (j - 4 * I), 128 * (j - 4 * I) + 128))]
                        t["cr"] = (128 * (j - 4 * I), 512)
                    tl.append(t)
                return tl

            def fin(I, ob, h=h):
                d = self.fin_norm(env, ob, 512)
                self.store_ot(d, 512, h * 64, I)

            self.attn_job(env, s["q"], 512, self.cfg.get("nblk", NBLK), tiles, fin)
        self.attn_flush(env)
        P.barrier()
        A.reset(mk)

    def swa_phase(self, l):
        P, A = self.P, self.A
        mk = A.mark()
        env = self.attn_env()
        env["act_copy"] = True
        hb = self.head_bufs(2)
        ms = self.cload("m_swa", [128, 2, 128], BF16)
        bs = self.cload("b_swa", [128, 4, 2], F32)
        fs = self.cload("f_swa", [64, 4, 512], F32)
        es = A.alloc([64, 4], F32, "esink")
        self.load(es, self.sinks[l])
        P.op("act", lambda e: e.activation(out=es[:, :], in_=es[:, :], func=AF.Exp), reads=[es.tok], writes=[es.tok])
        ZVAv = self.ZVA.rearrange("(s p) c -> p s c", p=128)
        nb = self.cfg.get("nblk", NBLK) * 4
        for h in range(4):
            s = hb[h % 2]
            half = h % 2
            kv = h // 2
            kp = s["kt"] if half == 0 else s["kb"]
            self.load_q(s["q"], 10 + h // 2)
            self.load_k(kp, 12, kv, half)
            self.load_v(s["v"], ZVAv, 384 + kv * 64)

            def tiles(I, s=s, kp=kp, h=h):
                tl = []
                for ri, r in enumerate((-1, 0)):
                    j = I + r
                    if j < 0:
                        continue
                    tl.append({"k": kp[:, j * 128:(j + 1) * 128], "ktoks": [kp.tok], "v": s["v"][:, j, :],
                               "vtoks": [s["v"].tok], "bias": bs[:, h, ri:ri + 1], "btoks": [bs.tok],
                               "adds": [(self.ident_bf[:, :], ms[:, ri, :], [self.ident_bf.tok, ms.tok])]})
                return tl

            def fin(I, ob, h=h):
                def hook(dcp):
                    P.op("dve", lambda e: e.scalar_tensor_tensor(
                        out=dcp[:, 0:512], in0=fs[:, h, :], scalar=es[:, h:h + 1], in1=dcp[:, 0:512],
                        op0=ALU.mult, op1=ALU.add), reads=[fs.tok, es.tok, dcp.tok], writes=[dcp.tok])
                d = self.fin_norm(env, ob, 512, hook=hook)
                self.store_ot(d, 512, (8 + h) * 64, I)

            self.attn_job(env, s["q"], 128, nb, tiles, fin, ogroup=4)
        self.attn_flush(env)
        P.barrier()
        A.reset(mk)

    def dil_phase(self, l):
        P, A = self.P, self.A
        mk = A.mark()
        env = self.attn_env()
        env["act_copy"] = True
        hb = self.head_bufs(2)
        wd = self.cload("w_dil", [128, 20, 512], BF16)
        bd = self.cload("b_dil", [128, 4, 20], F32)
        ZVBv = self.ZVB.rearrange("(s p) c -> p s c", p=128)
        for h in range(4):
            s = hb[h % 2]
            half = h % 2
            kp = s["kt"] if half == 0 else s["kb"]
            self.load_q(s["q"], 13 + h // 2)
            self.load_k(kp, 15 + h // 2, half, half)
            self.load_v(s["v"], ZVBv, h * 64)

            def tiles(I, s=s, kp=kp, h=h):
                tl = []
                for ri, r in enumerate(range(-16, 4)):
                    j = 4 * I + r
                    if j < 0:
                        continue
                    tl.append({"k": kp[:, j * 128:(j + 1) * 128], "ktoks": [kp.tok], "v": s["v"][:, j, :],
                               "vtoks": [s["v"].tok], "bias": bd[:, h, ri:ri + 1], "btoks": [bd.tok],
                               "mult": (wd[:, ri, :], wd.tok),
                               "cr": ((128 * r, 512) if r >= 0 else (0, min(512, 128 * (r + 17))))})
                return tl

            def fin(I, ob, h=h):
                d = self.fin_norm(env, ob, 512)
                self.store_ot(d, 512, (12 + h) * 64, I)

            self.attn_job(env, s["q"], 512, self.cfg.get("nblk", NBLK), tiles, fin)
        self.attn_flush(env)
        P.barrier()
        A.reset(mk)

    def nsa_band_phase(self, l, kind):
        P, A = self.P, self.A
        mk = A.mark()
        env = self.attn_env()
        env["act_copy"] = (kind == "win")
        hb = self.head_bufs(1)[0]
        q2 = A.alloc([128, T], BF16, "q2")
        qs = [hb["q"], q2]
        self.load_q(qs[0], 4)
        self.load_q(qs[1], 5)
        kch = 8 if kind == "slc" else 9
        self.load_k(hb["kt"], kch, 0, 0)
        self.load_k(hb["kb"], kch, 1, 1)
        ZVAv = self.ZVA.rearrange("(s p) c -> p s c", p=128)
        self.load_v(hb["v"], ZVAv, 256 if kind == "slc" else 320)
        if kind == "slc":
            mc = self.cload("m_causal", [128, 4, 512], BF16)
            bt = self.cload("b_slc", [128, 4, 32], F32)
            ex = self.cload("expand", [128, 32, 128], BF16)
        else:
            mw = self.cload("m_win", [128, 8, 512], BF16)
            bt = self.cload("b_win", [128, 4, 8], F32)
        gt = [A.alloc([64, 512], F32, "gt%d" % i) for i in range(2)]
        pt = [A.alloc([64, 512], F32, "pt%d" % i) for i in range(2)]
        cnt = [0]
        gbase = 0 if kind == "slc" else 2
        psrc = self.OCT if kind == "slc" else self.PART
        for h in range(4):
            half = h % 2
            kp = hb["kt"] if half == 0 else hb["kb"]
            q = qs[h // 2]

            def tiles(I, kp=kp, h=h):
                tl = []
                if kind == "slc":
                    for j in range(4 * I + 4):
                        rel = j - 4 * I
                        adds = [(ex[:, j, :], self.negselT[:, I * 512:(I + 1) * 512], [ex.tok, self.negselT.tok])]
                        if rel >= 0:
                            adds.append((self.ident_bf[:, :], mc[:, rel, :], [self.ident_bf.tok, mc.tok],
                                         (128 * rel, 128 * rel + 128)))
                        tl.append({"k": kp[:, j * 128:(j + 1) * 128], "ktoks": [kp.tok], "v": hb["v"][:, j, :],
                                   "vtoks": [hb["v"].tok], "bias": bt[:, h, rel + 28:rel + 29], "btoks": [bt.tok],
                                   "adds": adds, "cr": ((128 * rel, 512) if rel >= 0 else (0, 512))})
                else:
                    for ri, r in enumerate(range(-4, 4)):
                        j = 4 * I + r
                        if j < 0:
                            continue
                        tl.append({"k": kp[:, j * 128:(j + 1) * 128], "ktoks": [kp.tok], "v": hb["v"][:, j, :],
                                   "vtoks": [hb["v"].tok], "bias": bt[:, h, ri:ri + 1], "btoks": [bt.tok],
                                   "adds": [(self.ident_bf[:, :], mw[:, ri, :], [self.ident_bf.tok, mw.tok],
                                             ((128 * r, 128 * r + 128) if r >= 0 else (128 * (r + 4), 128 * (r + 5))))],
                                   "cr": ((128 * r, 512) if r >= 0 else (0, min(512, 128 * (r + 5))))})
                return tl

            def fin(I, ob, h=h):
                k = cnt[0] % 2
                cnt[0] += 1
                g, pp = gt[k], pt[k]
                cs = slice(I * 512, (I + 1) * 512)
                gr0 = (gbase + h // 2) * 128 + (h % 2) * 64
                P.dma("sp", lambda e: e.dma_start(out=g[:, :], in_=self.GT[gr0:gr0 + 64, cs]), writes=[g.tok])
                P.dma("sp", lambda e: e.dma_start(out=pp[:, :], in_=psrc[h * 64:(h + 1) * 64, cs]), writes=[pp.tok])
                d = self.fin_norm(env, ob, 512, out32=True)
                eng2 = "pool" if kind == "win" else "dve"
                P.op(eng2, lambda e: e.tensor_tensor(out=d[:, :], in0=d[:, :], in1=g[:, :], op=ALU.mult),
                     reads=[d.tok, g.tok], writes=[d.tok])
                if kind == "slc":
                    P.op("dve", lambda e: e.tensor_tensor(out=d[:, :], in0=d[:, :], in1=pp[:, :], op=ALU.add),
                         reads=[d.tok, pp.tok], writes=[d.tok])
                    P.dma("sp", lambda e: e.dma_start(out=self.PART[h * 64:(h + 1) * 64, cs], in_=d[:, :]),
                          reads=[d.tok])
                else:
                    ob16 = env["osb"][k]
                    P.op("pool", lambda e: e.tensor_tensor(out=ob16[:, :], in0=d[:, :], in1=pp[:, :], op=ALU.add),
                         reads=[d.tok, pp.tok], writes=[ob16.tok])
                    self.store_ot(ob16, 512, (4 + h) * 64, I)

            self.attn_job(env, q, 512, self.cfg.get("nblk", NBLK), tiles, fin)
        self.attn_flush(env)
        P.barrier()
        A.reset(mk)

    def nsa_cmp_phase(self, l):
        P, A = self.P, self.A
        mk = A.mark()
        q = [A.alloc([128, T], BF16, "cq%d" % i) for i in range(2)]
        self.load_q(q[0], 4)
        self.load_q(q[1], 5)
        mcmp = self.cload("m_cmp", [128, 17, 128], BF16)
        bcmp = self.cload("b_cmp", [128, 4, 48], F32)
        rcmp = self.cload("r_cmp", [128, 2, 130], BF16)
        sa = self.cload("sel_a", [128, 32, 64], F32)
        sbt = self.cload("sel_b", [128, 32, 64], F32)
        scz = self.cload("sel_c", [128, 32, 64], F32)
        identf = self.cload("ident_f", [128, 128], F32)
        gsb = A.alloc([128, NT, 16], F32, "gsbt")
        self.load(gsb, self.ZF.rearrange("(s p) c -> p s c", p=128))
        P.op("act", lambda e: e.activation(out=gsb[:, :, :], in_=gsb[:, :, :], func=AF.Sigmoid), reads=[gsb.tok],
             writes=[gsb.tok])
        kcp = [A.alloc([128, 256], BF16, "kcp%d" % i) for i in range(2)]
        P.op("pool", lambda e: e.memset(kcp[0][64:128, :], 0.0), writes=[kcp[0].tok])
        P.op("pool", lambda e: e.memset(kcp[1][0:64, :], 0.0), writes=[kcp[1].tok])
        for idx in range(2):
            kk = A.alloc([128, T + 32], BF16, "kk%d" % idx)
            P.op("pool", lambda e, kk=kk: e.memset(kk[:, T - 32:T + 32], 0.0), writes=[kk.tok])
            r0 = (6 + idx) * 128
            P.dma("sp", lambda e, kk=kk, r0=r0: e.dma_start(out=kk[0:64, 0:T], in_=self.ZT[r0:r0 + 64, 0:T]),
                  writes=[kk.tok])
            P.dma("sp", lambda e, kk=kk, r0=r0: e.dma_start(out=kk[64:128, 0:T - 1], in_=self.ZT[r0 + 64:r0 + 128, 1:T]),
                  writes=[kk.tok])
            w1 = A.alloc([128, 16, 128], BF16, "w1_%d" % idx)
            P.dma("pool", lambda e, w1=w1, idx=idx: e.dma_start(
                out=w1[:, :, :], in_=self.cmp_w1[l, idx].rearrange("(lp p) e -> p lp e", p=128)), writes=[w1.tok])
            pe = A.alloc([128, 16], BF16, "pe_%d" % idx)
            P.dma("pool", lambda e, pe=pe, idx=idx: e.dma_start(out=pe[:, :], in_=self.cmp_pe[l, idx]),
                  writes=[pe.tok])
            w2 = A.alloc([128, 128], BF16, "w2_%d" % idx)
            P.dma("pool", lambda e, w2=w2, idx=idx: e.dma_start(out=w2[:, :], in_=self.cmp_w2[l, idx]),
                  writes=[w2.tok])
            pcst, ph = self.ps[4], self.ps[5]
            for lp in range(16):
                P.op("pe", lambda e, lp=lp, w1=w1, pe=pe: e.matmul(pcst[:, 0:1], lhsT=w1[:, lp, :], rhs=pe[:, lp:lp + 1],
                                                                   start=(lp == 0), stop=(lp == 15)),
                     reads=[w1.tok, pe.tok], writes=[pcst.tok])
            cst = A.alloc([128, 1], F32, "cst%d" % idx)
            P.op("dve", lambda e, cst=cst: e.tensor_copy(out=cst[:, :], in_=pcst[:, 0:1]), reads=[pcst.tok],
                 writes=[cst.tok])
            for lp in range(16):
                P.op("pe", lambda e, lp=lp, w1=w1, kk=kk: e.matmul(
                    ph[:, 0:256], lhsT=w1[:, lp, :], rhs=kk[:, 2 * lp:2 * lp + 4096:16], start=(lp == 0),
                    stop=(lp == 15)), reads=[w1.tok, kk.tok], writes=[ph.tok])
            hid = A.alloc([128, 256], BF16, "hid%d" % idx)
            P.op("act", lambda e, hid=hid, cst=cst: e.activation(out=hid[:, :], in_=ph[:, 0:256], func=AF.Silu,
                                                                 bias=cst[:, 0:1]),
                 reads=[ph.tok, cst.tok], writes=[hid.tok])
            if idx == 0:
                pk = self.ps[6]
                P.op("pe", lambda e, w2=w2, hid=hid: e.matmul(pk[:, 0:256], lhsT=w2[:, :], rhs=hid[:, :], start=True,
                                                              stop=True), reads=[w2.tok, hid.tok], writes=[pk.tok])
                P.op("dve", lambda e: e.tensor_copy(out=kcp[0][0:64, :], in_=pk[0:64, 0:256]), reads=[pk.tok],
                     writes=[kcp[0].tok])
                P.op("dve", lambda e: e.tensor_copy(out=kcp[1][64:128, :], in_=pk[64:128, 0:256]), reads=[pk.tok],
                     writes=[kcp[1].tok])
            else:
                for ch in range(2):
                    pv = self.ps[6]
                    P.op("pe", lambda e, ch=ch, w2=w2, hid=hid: e.matmul(
                        pv[:, 0:64], lhsT=hid[:, ch * 128:(ch + 1) * 128], rhs=w2[:, 0:64], start=True, stop=True),
                        reads=[w2.tok, hid.tok], writes=[pv.tok])
                    P.op("dve", lambda e, ch=ch: e.tensor_copy(out=rcmp[:, ch, 65:129], in_=pv[:, 0:64]),
                         reads=[pv.tok], writes=[rcmp.tok])
        pslot = [A.alloc([128, 128], BF16, "cps%d" % i) for i in range(3)]
        imp = A.alloc([128, 64], F32, "imp")
        ocmp = A.alloc([128, 256], F32, "ocmp")
        rd = A.alloc([128, 4], F32, "rd")
        rd2 = A.alloc([128, 4], F32, "rd2")
        sc = A.alloc([128, 64], F32, "sc")
        sc2 = A.alloc([128, 64], F32, "sc2")
        mx = A.alloc([128, 16], F32, "mx")
        negs = A.alloc([128, 128], BF16, "negs")
        octs = A.alloc([128, 2, 128], F32, "octs")
        P.op("pool", lambda e: e.memset(negs[:, :], 0.0), writes=[negs.tok])
        OCTv = self.OCT.rearrange("(k p) t -> p k t", p=128)
        npc = 0
        nsb = 0
        sbanks = [self.ps[0], self.ps[1], self.ps[6]]
        pslot = pslot + [A.alloc([128, 128], BF16, "cps%d" % (3 + i)) for i in range(3)]
        pend = []

        def stage_b(tt, h, pC, items):
            for ci, (ch, psl) in enumerate(items):
                P.op("pe", lambda e, pC=pC, psl=psl, ch=ch, ci=ci, n=len(items): e.matmul(
                    pC[:, 0:129], lhsT=psl[:, :], rhs=rcmp[:, ch, 0:129], start=(ci == 0),
                    stop=(ci == n - 1)), reads=[psl.tok, rcmp.tok], writes=[pC.tok])
            P.op("dve", lambda e, pC=pC, h=h: e.tensor_scalar_max(out=rd[:, h:h + 1], in0=pC[:, 64:65],
                                                                  scalar1=1e-30), reads=[pC.tok], writes=[rd.tok])
            P.op("dve", lambda e, h=h: e.reciprocal(out=rd[:, h:h + 1], in_=rd[:, h:h + 1]), reads=[rd.tok],
                 writes=[rd.tok])
            if h == 0:
                P.op("dve", lambda e, pC=pC, h=h: e.tensor_scalar(out=imp[:, :], in0=pC[:, 0:64],
                                                                  scalar1=rd[:, h:h + 1], scalar2=None,
                                                                  op0=ALU.mult),
                     reads=[pC.tok, rd.tok], writes=[imp.tok])
            else:
                P.op("dve", lambda e, pC=pC, h=h: e.scalar_tensor_tensor(
                    out=imp[:, :], in0=pC[:, 0:64], scalar=rd[:, h:h + 1], in1=imp[:, :], op0=ALU.mult,
                    op1=ALU.add), reads=[pC.tok, rd.tok, imp.tok], writes=[imp.tok])
            P.op("dve", lambda e, h=h, tt=tt: e.tensor_tensor(out=rd2[:, h:h + 1], in0=rd[:, h:h + 1],
                                                              in1=gsb[:, tt, 4 + 3 * h:5 + 3 * h], op=ALU.mult),
                 reads=[rd.tok, gsb.tok], writes=[rd2.tok])
            P.op("act", lambda e, pC=pC, h=h: e.activation(out=ocmp[:, h * 64:(h + 1) * 64], in_=pC[:, 65:129],
                                                           func=AF.Copy, scale=rd2[:, h:h + 1]),
                 reads=[pC.tok, rd2.tok], writes=[ocmp.tok])
            if h == 3:
                epilogue(tt)

        def epilogue(tt):
            ts = slice(tt * 128, (tt + 1) * 128)
            for k in range(2):
                ptp = self.ps[4 + k]
                P.op("pe", lambda e, ptp=ptp, k=k: e.transpose(ptp[:, 0:128], ocmp[:, k * 128:(k + 1) * 128],
                                                               identf[:, :]),
                     reads=[ocmp.tok, identf.tok], writes=[ptp.tok])
                P.op("act", lambda e, ptp=ptp, k=k: e.activation(out=octs[:, k, :], in_=ptp[:, 0:128], func=AF.Copy),
                     reads=[ptp.tok], writes=[octs.tok])
            P.dma("sp", lambda e, ts=ts: e.dma_start(out=OCTv[:, :, ts], in_=octs[:, :, :]), reads=[octs.tok])
            P.op("dve", lambda e, tt=tt: e.tensor_tensor(out=sc[:, :], in0=imp[:, :], in1=sa[:, tt, :], op=ALU.mult),
                 reads=[imp.tok, sa.tok], writes=[sc.tok])
            P.op("dve", lambda e, tt=tt: e.tensor_tensor(out=sc[:, :], in0=sc[:, :], in1=sbt[:, tt, :], op=ALU.add),
                 reads=[sc.tok, sbt.tok], writes=[sc.tok])
            P.op("dve", lambda e: e.max(out=mx[:, 0:8], in_=sc[:, :]), reads=[sc.tok], writes=[mx.tok])
            P.op("dve", lambda e: e.match_replace(out=sc2[:, :], in_to_replace=mx[:, 0:8], in_values=sc[:, :],
                                                  imm_value=-100.0), reads=[sc.tok, mx.tok], writes=[sc2.tok])
            P.op("dve", lambda e: e.max(out=mx[:, 8:16], in_=sc2[:, :]), reads=[sc2.tok], writes=[mx.tok])
            P.op("dve", lambda e: e.tensor_scalar(out=sc2[:, :], in0=sc[:, :], scalar1=mx[:, 15:16], scalar2=None,
                                                  op0=ALU.is_ge), reads=[sc.tok, mx.tok], writes=[sc2.tok])
            P.op("dve", lambda e, tt=tt: e.tensor_tensor(out=sc2[:, :], in0=sc2[:, :], in1=scz[:, tt, :],
                                                         op=ALU.mult), reads=[sc2.tok, scz.tok], writes=[sc2.tok])
            P.op("dve", lambda e: e.tensor_scalar(out=negs[:, 0:64], in0=sc2[:, :], scalar1=-1.0, scalar2=-NEG,
                                                  op0=ALU.add, op1=ALU.mult), reads=[sc2.tok], writes=[negs.tok])
            P.op("pe", lambda e: e.transpose(self.psb[:, 0:128], negs[:, :], self.ident_bf[:, :]),
                 reads=[negs.tok, self.ident_bf.tok], writes=[self.psb.tok])
            nsT = self.negselT
            P.op("act", lambda e, ts=ts, nsT=nsT: e.activation(out=nsT[:, ts], in_=self.psb[:, 0:128], func=AF.Copy),
                 reads=[self.psb.tok], writes=[nsT.tok])

        for tt in range(self.cfg.get("nblk", NBLK) * 4):
            ts = slice(tt * 128, (tt + 1) * 128)
            for h in range(4):
                qq = q[h // 2]
                kc_ = kcp[h % 2]
                pC = self.ps[2 + h % 2]
                chs = [0] + ([1] if tt >= 16 else [])
                items = []
                for ci, ch in enumerate(chs):
                    di = tt - 16 * ch
                    sb = sbanks[nsb % 3]
                    nsb += 1
                    psl = pslot[npc % 6]
                    npc += 1
                    need_mask = di <= 16
                    P.op("pe", lambda e, sb=sb, kc_=kc_, ch=ch, qq=qq, ts=ts, need_mask=need_mask: e.matmul(
                        sb[:, 0:128], lhsT=kc_[:, ch * 128:(ch + 1) * 128], rhs=qq[:, ts], start=True,
                        stop=(not need_mask)), reads=[kc_.tok, qq.tok], writes=[sb.tok])
                    if need_mask:
                        P.op("pe", lambda e, sb=sb, di=di: e.matmul(sb[:, 0:128], lhsT=self.ident_bf[:, :],
                                                                    rhs=mcmp[:, di, :], start=False, stop=True),
                             reads=[self.ident_bf.tok, mcmp.tok], writes=[sb.tok])
                    P.op("act", lambda e, sb=sb, psl=psl, h=h, di=di: e.activation(
                        out=psl[:, :], in_=sb[:, 0:128], func=AF.Exp, bias=bcmp[:, h, di + 16:di + 17], scale=0.125),
                        reads=[sb.tok, bcmp.tok], writes=[psl.tok])
                    items.append((ch, psl))
                pend.append((tt, h, pC, items))
                if len(pend) > 2:
                    stage_b(*pend.pop(0))
        while pend:
            stage_b(*pend.pop(0))
        P.barrier()
        A.reset(mk)

    def outproj_phase(self, l, src):
        P, A = self.P, self.A
        mk = A.mark()
        wo = A.alloc([128, NK, D], BF16, "wo")
        wov = self.WOUTB[l].rearrange("(c p) n -> p c n", p=128)
        P.dma("act", lambda e: e.dma_start(out=wo[:, :, :], in_=wov[:, :, :]), reads=[self.tok_woutb[l]], writes=[wo.tok])
        xbs = [A.alloc([128, NK, TB], F32, "xb%d" % i) for i in range(2)]
        ots = [A.alloc([128, NK, TB], BF16, "otb%d" % i) for i in range(2)]
        srcv = src.rearrange("(c p) t -> p c t", p=128)
        dstv = self.XT.rearrange("(c p) t -> p c t", p=128)
        OTv = self.OT.rearrange("(c p) t -> p c t", p=128)
        nblk = self.cfg.get("nblk", NBLK)

        def ld(blk):
            cs = slice(blk * TB, (blk + 1) * TB)
            xb, ot = xbs[blk % 2], ots[blk % 2]
            P.dma("sp", lambda e: e.dma_start(out=ot[:, :, :], in_=OTv[:, :, cs]), writes=[ot.tok])
            P.dma("sp", lambda e: e.dma_start(out=xb[:, :, :], in_=srcv[:, :, cs]), writes=[xb.tok])

        ld(0)
        for blk in range(nblk):
            cs = slice(blk * TB, (blk + 1) * TB)
            xb, ot = xbs[blk % 2], ots[blk % 2]
            if blk + 1 < nblk:
                ld(blk + 1)
            for n in range(NK):
                py = self.ps[4 + n % 2]
                for mc_ in range(NK):
                    P.op("pe", lambda e, py=py, mc_=mc_, n=n, ot=ot: e.matmul(
                        py[:, :], lhsT=wo[:, mc_, n * 128:(n + 1) * 128], rhs=ot[:, mc_, :], start=(mc_ == 0),
                        stop=(mc_ == NK - 1)), reads=[wo.tok, ot.tok], writes=[py.tok])
                P.op("dve", lambda e, py=py, n=n, xb=xb: e.tensor_tensor(out=xb[:, n, :], in0=py[:, :], in1=xb[:, n, :],
                                                                  op=ALU.add), reads=[py.tok, xb.tok], writes=[xb.tok])
            P.dma("sp", lambda e, cs=cs, xb=xb: e.dma_start(out=dstv[:, :, cs], in_=xb[:, :, :]), reads=[xb.tok])
        P.barrier()
        A.reset(mk)


_COLS = None


def prep_shared(inp):
    global _COLS
    if _COLS is None:
        _COLS = win_col_layout()
    f32 = np.float32
    sh = {}
    gains = np.stack([np.stack([inp["norm_ffn1"][l], inp["norm_mix"][l], inp["norm_ffn2"][l]]) for l in range(L)])
    gains = np.concatenate([gains.reshape(L * 3, D), inp["norm_final"].reshape(1, D)], axis=0)
    sh["norms"] = np.ascontiguousarray(gains.reshape(L * 3 + 1, NK, 128).transpose(0, 2, 1)).astype(f32)
    for nm in ("ffn1_w_gate", "ffn1_w_up", "ffn2_w_gate", "ffn2_w_up", "ffn1_w_down", "ffn2_w_down", "w_out"):
        sh[nm] = np.ascontiguousarray(inp[nm], dtype=f32)
    sh["w_in_ext"] = np.ascontiguousarray(np.asarray(inp["w_in"], f32)[:, :, _COLS])
    sh["fox_b_rep"] = np.ascontiguousarray(
        np.broadcast_to(np.asarray(inp["fox_b_f"], f32)[:, None, None, :], (L, 128, NT, 4)).reshape(L, 128, 128))
    pe = np.asarray(inp["nsa_cmp_pe"], f32)
    sh["cmp_pe_l"] = np.ascontiguousarray(pe.reshape(L, 2, 16, 128).transpose(0, 1, 3, 2))
    sh["nsa_cmp_w1"] = np.ascontiguousarray(np.asarray(inp["nsa_cmp_w1"], f32).reshape(L, 2, 2048, 128))
    w2 = np.asarray(inp["nsa_cmp_w2"], f32)
    sh["cmp_w2_dup"] = np.ascontiguousarray(np.concatenate([w2, w2], axis=-1))
    sh["sinks_rep"] = np.ascontiguousarray(
        np.broadcast_to(np.asarray(inp["swa_sinks"], f32)[:, None, :], (L, 64, 4)))
    for k, v in const_specs().items():
        sh["c_" + k] = v
    return sh


_NC_CACHE = {}


def get_program(cfg_key=None, cfg=None):
    key = cfg_key or "full"
    if key not in _NC_CACHE:
        b = Builder(cfg)
        _NC_CACHE[key] = (b.build(), b)
    return _NC_CACHE[key]


def kernel(**inputs):
    x = np.asarray(inputs["x"], np.float32)
    B = x.shape[0]
    nc, b = get_program()
    sh = prep_shared(inputs)
    in_maps = []
    for c in range(B):
        m = dict(sh)
        m["xT"] = np.ascontiguousarray(x[c].T)
        in_maps.append({k: m[k] for k in b.inputs})
    res = run_bass_kernel_spmd(nc, in_maps, core_ids=list(range(B)))
    out = np.stack([np.ascontiguousarray(np.asarray(r["yT"]).T) for r in res.results], axis=0)
    return out.astype(np.float32)
```

```python
import contextlib
import numpy as np
import concourse.bass as bass
import concourse.mybir as mybir
from concourse.bass_utils import run_bass_kernel_spmd

F32 = mybir.dt.float32
BF16 = mybir.dt.bfloat16
AF = mybir.ActivationFunctionType
ALU = mybir.AluOpType

D = 1024
T = 4096
FF = 2816
NF = 22
NK = 8
TB = 512
NBLK = 8
L = 2
NT = 32
NEG = -30000.0
EPS = 1e-6
NFM = 21
NCOLS = NFM * 128 + 512 + 272

ENGS = ("pe", "act", "dve", "pool", "sp")
SAME_ENGINE_SYNC = True
NDMASEM = 12
NDMASEM_Q = {"pool": 8, "sp": 12}


class Tok:
    __slots__ = ("name", "w", "r", "rd")

    def __init__(self, name=""):
        self.name = name
        self.w = None
        self.r = {}
        self.rd = []


class Ins:
    __slots__ = ("eng", "fn", "deps", "dma", "sem", "val", "signal", "count", "prev_dma")

    def __init__(self, eng, fn, dma=False):
        self.eng = eng
        self.fn = fn
        self.deps = []
        self.dma = dma
        self.sem = None
        self.val = 0
        self.signal = False
        self.count = 0
        self.prev_dma = None


class Prog:
    def __init__(self, nc):
        self.nc = nc
        self.q = {e: [] for e in ENGS}
        self.ndma = {e: 0 for e in ENGS}
        self.dma_last = {}
        self.bar = {e: [] for e in ENGS}
        self.all_dma_since_bar = []

    def _deps(self, ins, reads, writes):
        deps = []
        seen = set()

        def add(d):
            if d is None or d is ins or id(d) in seen:
                return
            seen.add(id(d))
            deps.append(d)

        for t in reads:
            add(t.w)
        for t in writes:
            add(t.w)
            for r in t.r.values():
                add(r)
            for r in t.rd:
                add(r)
        for d in self.bar[ins.eng]:
            add(d)
        self.bar[ins.eng] = []
        ins.deps = deps
        for t in reads:
            if ins.dma:
                t.rd.append(ins)
            else:
                t.r[ins.eng] = ins
        for t in writes:
            t.w = ins
            t.r = {}
            t.rd = []

    def op(self, eng, fn, reads=(), writes=()):
        ins = Ins(eng, fn)
        self._deps(ins, reads, writes)
        self.q[eng].append(ins)
        return ins

    def dma(self, eng, fn, reads=(), writes=()):
        ins = Ins(eng, fn, dma=True)
        i = self.ndma[eng]
        self.ndma[eng] += 1
        nq = NDMASEM_Q.get(eng, NDMASEM)
        ins.sem = (eng, i % nq)
        ins.val = 16 * (i // nq + 1)
        ins.prev_dma = self.dma_last.get(ins.sem)
        self.dma_last[ins.sem] = ins
        self._deps(ins, reads, writes)
        self.q[eng].append(ins)
        self.all_dma_since_bar.append(ins)
        return ins

    def barrier(self):
        deps = []
        for e in ENGS:
            for ins in reversed(self.q[e]):
                if not ins.dma and ins.fn is not None:
                    deps.append(ins)
                    break
        deps += self.all_dma_since_bar
        self.all_dma_since_bar = []
        for e in ENGS:
            self.bar[e] = self.bar[e] + deps

    def finish(self):
        self.barrier()
        for e in ENGS:
            self.op(e, None)

    def emit(self):
        nc = self.nc
        for e in ENGS:
            for ins in self.q[e]:
                for d in ins.deps:
                    if not d.dma:
                        if d.eng == ins.eng and (d.eng == "pe" or not SAME_ENGINE_SYNC):
                            continue
                        d.signal = True
        for e in ENGS:
            c = 0
            for ins in self.q[e]:
                if ins.signal:
                    c += 1
                ins.count = c
        with contextlib.ExitStack() as st:
            S = {e: st.enter_context(nc.semaphore("s_" + e)) for e in ENGS}
            Dm = {}
            for e in ENGS:
                for k in range(min(NDMASEM, self.ndma[e])):
                    Dm[(e, k)] = st.enter_context(nc.semaphore("d_%s%d" % (e, k)))
            block = st.enter_context(nc.Block())
            engobj = {"pe": "tensor", "act": "scalar", "dve": "vector", "pool": "gpsimd", "sp": "sync"}

            def run(ename, eng):
                waited = {}
                for ins in self.q[ename]:
                    deps = list(ins.deps)
                    if ins.dma and ins.prev_dma is not None:
                        deps.append(ins.prev_dma)
                    for d in deps:
                        if d.dma:
                            key = d.sem
                            val = d.val
                            sem = Dm[d.sem]
                        else:
                            if d.eng == ename and (ename == "pe" or not SAME_ENGINE_SYNC):
                                continue
                            key = d.eng
                            val = d.count
                            sem = S[d.eng]
                        if waited.get(key, 0) >= val:
                            continue
                        waited[key] = val
                        eng.wait_ge(sem, val)
                    if ins.fn is None:
                        continue
                    r = ins.fn(eng)
                    if ins.dma:
                        r.then_inc(Dm[ins.sem], 16)
                    elif ins.signal:
                        r.then_inc(S[ename], 1)

            for ename in ENGS:
                if not self.q[ename]:
                    continue
                getattr(block, engobj[ename])(lambda eng, ename=ename: run(ename, eng))


class Buf:
    def __init__(self, t, name=""):
        self.t = t
        self.tok = Tok(name)

    def __getitem__(self, k):
        return self.t[k]


class Arena:
    def __init__(self, nc, lo=16512, hi=229344):
        self.nc = nc
        self.lo = lo
        self.hi = hi
        self.cur = lo
        self.n = 0

    def alloc(self, shape, dtype, name="b"):
        esz = 4 if dtype == F32 else 2
        per = 1
        for s in shape[1:]:
            per *= s
        nbytes = per * esz
        off = (self.cur + 63) // 64 * 64
        assert off + nbytes <= self.hi, ("SBUF overflow", name, off, nbytes)
        self.cur = off + nbytes
        self.n += 1
        t = self.nc.alloc_sbuf_tensor_at("%s_%d" % (name, self.n), list(shape), dtype, offset=off)
        return Buf(t, name)

    def mark(self):
        return self.cur

    def reset(self, m):
        self.cur = m


def alibi_slopes():
    n = 12
    return (2.0 ** (-8.0 * np.arange(1, n + 1) / n)).astype(np.float64)


def make_consts():
    import ml_dtypes

    bf = ml_dtypes.bfloat16
    c = {}
    sl = alibi_slopes()
    sl_swa, sl_nsa, sl_dil = sl[0:4], sl[4:8], sl[8:12]
    p = np.arange(128)
    c["ident_bf"] = np.eye(128, dtype=np.float32).astype(bf)
    c["ones_bf"] = np.ones((128, 128), np.float32).astype(bf)
    c["ident_f"] = np.eye(128, dtype=np.float32)
    c["ones_f"] = np.ones((128, 128), np.float32)
    c["tri_f"] = (p[:, None] <= p[None, :]).astype(np.float32)
    tq = np.arange(512)
    m = np.zeros((128, 4, 512), np.float32)
    for r in range(4):
        m[:, r, :] = np.where(128 * r + p[:, None] <= tq[None, :], 0.0, NEG)
    c["m_causal"] = m.astype(bf)
    m = np.zeros((128, 8, 512), np.float32)
    for ri, r in enumerate(range(-4, 4)):
        dist = tq[None, :] - (128 * r + p[:, None])
        m[:, ri, :] = np.where((dist >= 0) & (dist <= 511), 0.0, NEG)
    c["m_win"] = m.astype(bf)
    b = np.zeros((128, 4, 8), np.float32)
    for h in range(4):
        for ri, r in enumerate(range(-4, 4)):
            b[:, h, ri] = sl_nsa[h] * (128 * r + p - 255)
    c["b_win"] = b
    b = np.zeros((128, 4, 32), np.float32)
    for h in range(4):
        for ri, r in enumerate(range(-28, 4)):
            b[:, h, ri] = np.maximum(sl_nsa[h] * (128 * r + p - 255), -3.0e4)
    c["b_slc"] = b
    e = np.zeros((128, 32, 128), np.float32)
    for j in range(32):
        for pp in range(128):
            e[2 * j + pp // 64, j, pp] = 1.0
    c["expand"] = e.astype(bf)
    tq1 = np.arange(128)
    m = np.zeros((128, 2, 128), np.float32)
    for ri, r in enumerate((-1, 0)):
        dist = tq1[None, :] - (128 * r + p[:, None])
        m[:, ri, :] = np.where((dist >= 0) & (dist <= 127), 0.0, NEG)
    c["m_swa"] = m.astype(bf)
    b = np.zeros((128, 4, 2), np.float32)
    f = np.zeros((64, 4, 512), np.float32)
    for h in range(4):
        for ri, r in enumerate((-1, 0)):
            b[:, h, ri] = sl_swa[h] * (128 * r + p - 63)
        f[:, h, :] = np.tile(np.exp(sl_swa[h] * (tq1 - 63)), 4)[None, :]
    c["b_swa"] = b
    c["f_swa"] = f
    w = np.zeros((128, 20, 512), np.float32)
    for ri, r in enumerate(range(-16, 4)):
        dist = tq[None, :] - (128 * r + p[:, None])
        mult = ((dist >= 0) & (dist <= 128)).astype(np.float32)
        mult += ((dist >= 0) & (dist % 4 == 0) & (dist // 4 <= 128)).astype(np.float32)
        mult += ((dist >= 0) & (dist % 16 == 0) & (dist // 16 <= 128)).astype(np.float32)
        w[:, ri, :] = mult
    c["w_dil"] = w.astype(bf)
    b = np.zeros((128, 4, 20), np.float32)
    for h in range(4):
        for ri, r in enumerate(range(-16, 4)):
            b[:, h, ri] = sl_dil[h] * (128 * r + p - 255)
    c["b_dil"] = b
    m = np.zeros((128, 17, 128), np.float32)
    for di in range(17):
        m[:, di, :] = np.where(tq1[None, :] + 128 * di >= 16 * p[:, None] + 31, 0.0, NEG)
    c["m_cmp"] = m.astype(bf)
    b = np.zeros((128, 4, 48), np.float32)
    for h in range(4):
        for di, dd in enumerate(range(-16, 32)):
            b[:, h, di] = np.maximum(sl_nsa[h] * (16 * p - 32 - 128 * dd), -3.0e4)
    c["b_cmp"] = b
    r = np.zeros((128, 2, 130), np.float32)
    for ch in range(2):
        for pp in range(128):
            n = 128 * ch + pp
            if n >= 255:
                continue
            for j in range(64):
                if 16 * n <= 64 * j + 63 and 16 * n + 31 >= 64 * j:
                    r[pp, ch, j] = 1.0
            r[pp, ch, 64] = 1.0
    c["r_cmp"] = r.astype(bf)
    A = np.zeros((128, 32, 64), np.float32)
    B = np.zeros((128, 32, 64), np.float32)
    Cz = np.zeros((128, 32, 64), np.float32)
    for tt in range(32):
        for pp in range(128):
            t = 128 * tt + pp
            cur = t // 64
            for j in range(64):
                causal = 64 * j <= t
                if not causal:
                    B[pp, tt, j] = -1.0 - 0.01 * j
                    continue
                Cz[pp, tt, j] = 1.0
                if j == 0:
                    B[pp, tt, j] = 12.0
                elif j == cur:
                    B[pp, tt, j] = 11.0
                elif j == cur - 1:
                    B[pp, tt, j] = 10.0
                else:
                    A[pp, tt, j] = 1.0
    c["sel_a"] = A
    c["sel_b"] = B
    c["sel_c"] = Cz
    return c


CONST_SPECS = None


def const_specs():
    global CONST_SPECS
    if CONST_SPECS is None:
        c = make_consts()
        CONST_SPECS = c
    return CONST_SPECS


def win_col_layout():
    off = {}
    o = 0
    splits = (("fox_q", 256), ("fox_k", 256), ("fox_v", 256), ("fox_f", 4), ("nsa_q", 256),
              ("nsa_k_cmp", 64), ("nsa_v_cmp", 64), ("nsa_k_slc", 64), ("nsa_v_slc", 64),
              ("nsa_k_win", 64), ("nsa_v_win", 64), ("nsa_gate", 12), ("swa_q", 256),
              ("swa_k", 128), ("swa_v", 128), ("dil_q", 256), ("dil_k", 256), ("dil_v", 256))
    for n, w in splits:
        off[n] = o
        o += w
    assert o == 2704

    def rng(n, a, b):
        return list(range(off[n] + a, off[n] + b))

    cols = []
    cols += rng("fox_q", 0, 256) + rng("fox_k", 0, 256)
    cols += rng("nsa_q", 0, 256)
    cols += rng("nsa_k_cmp", 0, 64) * 2 + rng("nsa_v_cmp", 0, 64) * 2
    cols += rng("nsa_k_slc", 0, 64) * 2 + rng("nsa_k_win", 0, 64) * 2
    cols += rng("swa_q", 0, 256) + rng("swa_k", 0, 128)
    cols += rng("dil_q", 0, 256) + rng("dil_k", 0, 256)
    g = off["nsa_gate"]
    for c_ in (1, 2):
        for h in range(4):
            cols += [g + 3 * h + c_] * 64
    assert len(cols) == NFM * 128
    cols += rng("fox_v", 0, 256) + rng("nsa_v_slc", 0, 64) + rng("nsa_v_win", 0, 64) + rng("swa_v", 0, 128)
    cols += rng("dil_v", 0, 256) + rng("fox_f", 0, 4) + rng("nsa_gate", 0, 12)
    assert len(cols) == NCOLS
    return np.asarray(cols, np.int64)


class Builder:
    def __init__(self, cfg=None):
        self.cfg = cfg or {}
        nc = bass.Bass("TRN2", target_bir_lowering=False)
        self.nc = nc
        self.P = Prog(nc)
        self.A = Arena(nc)
        self.inputs = {}
        self.ps = [Buf(nc.alloc_psum_tensor("ps%d" % i, [128, 512], F32), "ps%d" % i) for i in range(7)]
        self.psb = Buf(nc.alloc_psum_tensor("psb", [128, 1024], BF16), "psb")
        self.dbg = {}

    def din(self, name, shape, dtype=F32):
        ap = self.nc.dram_tensor(name, list(shape), dtype, kind="ExternalInput").ap()
        self.inputs[name] = ap
        return ap

    def dscr(self, name, shape, dtype):
        kind = "ExternalOutput" if name in self.cfg.get("dump", ()) else "Internal"
        ap = self.nc.dram_tensor(name, list(shape), dtype, kind=kind).ap()
        return ap

    def load(self, buf, src, eng="sp", view=None):
        dst = buf.t[:] if view is None else view
        return self.P.dma(eng, lambda e: e.dma_start(out=dst, in_=src), writes=[buf.tok])

    def declare(self):
        nc = self.nc
        self.xT = self.din("xT", [D, T])
        self.yT = nc.dram_tensor("yT", [D, T], F32, kind="ExternalOutput").ap()
        self.norms = self.din("norms", [L * 3 + 1, 128, NK])
        self.w_ffn = {}
        for nm in ("ffn1_w_gate", "ffn1_w_up", "ffn2_w_gate", "ffn2_w_up"):
            self.w_ffn[nm] = self.din(nm, [L, D, FF])
        for nm in ("ffn1_w_down", "ffn2_w_down"):
            self.w_ffn[nm] = self.din(nm, [L, FF, D])
        self.w_in = self.din("w_in_ext", [L, D, NCOLS])
        self.w_out = self.din("w_out", [L, D, D])
        self.fox_b = self.din("fox_b_rep", [L, 128, 128])
        self.cmp_pe = self.din("cmp_pe_l", [L, 2, 128, 16])
        self.cmp_w1 = self.din("nsa_cmp_w1", [L, 2, 2048, 128])
        self.cmp_w2 = self.din("cmp_w2_dup", [L, 2, 128, 128])
        self.sinks = self.din("sinks_rep", [L, 64, 4])
        cs = const_specs()
        self.cin = {}
        for k, v in cs.items():
            self.cin[k] = self.din("c_" + k, v.shape, BF16 if v.dtype != np.float32 else F32)
        self.XT = self.dscr("XT", [D, T], F32)
        self.ZT = self.dscr("ZT", [17 * 128, T + 32], BF16)
        self.GT = self.dscr("GT", [4 * 128, T], F32)
        self.ZVA = self.dscr("ZVA", [T, 512], BF16)
        self.ZVB = self.dscr("ZVB", [T, 256], BF16)
        self.ZF = self.dscr("ZF", [T, 16], F32)
        self.OT = self.dscr("OT", [D, T], BF16)
        self.OCT = self.dscr("OCT", [256, T], F32)
        self.PART = self.dscr("PART", [256, T], F32)
        self.HT = self.dscr("HT", [D, T], BF16)
        self.WINB = self.dscr("WINB", [L, D, NCOLS], BF16)
        self.WOUTB = self.dscr("WOUTB", [L, D, D], BF16)
        self.tok_winb = [Tok() for _ in range(L)]
        self.tok_woutb = [Tok() for _ in range(L)]
        self.precast_done = False

    def setup_consts(self):
        A = self.A
        self.ident_bf = A.alloc([128, 128], BF16, "ident_bf")
        self.ones_bf = A.alloc([128, 128], BF16, "ones_bf")
        self.load(self.ident_bf, self.cin["ident_bf"])
        self.load(self.ones_bf, self.cin["ones_bf"])
        self.eps_t = A.alloc([128, 1], F32, "eps_t")
        self.P.op("dve", lambda e: e.memset(self.eps_t[:, :], EPS), writes=[self.eps_t.tok])
        self.gains = A.alloc([128, L * 3 + 1, NK], F32, "gains")
        self.load(self.gains, self.norms.rearrange("g p k -> p g k"))
        zpad = A.alloc([128, 17, 32], BF16, "zpad")
        ZTv0 = self.ZT.rearrange("(c p) t -> p c t", p=128)
        self.P.op("pool", lambda e: e.memset(zpad[:, :, :], 0.0), writes=[zpad.tok])
        self.P.dma("sp", lambda e: e.dma_start(out=ZTv0[:, :, T:T + 32], in_=zpad[:, :, :]), reads=[zpad.tok])
        self.WA = self.alloc_half("wa")
        self.mark_wb = A.mark()

    def rms_block(self, xb, sq, hT, rstd, gi, out_f32=None, tmp=None, part="ab"):
        P = self.P
        ps = self.ps[6]
        if "a" in part:
            for kc in range(NK):
                P.op("act", lambda e, kc=kc: e.activation(out=sq[:, kc, :], in_=xb[:, kc, :], func=AF.Square),
                     reads=[xb.tok], writes=[sq.tok])
        if "b" not in part:
            return
        for kc in range(NK):
            P.op("pe", lambda e, kc=kc: e.matmul(ps[:, :], lhsT=self.ones_bf[:, :], rhs=sq[:, kc, :],
                                                  start=(kc == 0), stop=(kc == NK - 1)),
                 reads=[sq.tok, self.ones_bf.tok], writes=[ps.tok])
        P.op("act", lambda e: e.activation(out=rstd[:, :], in_=ps[:, :], func=AF.Sqrt, bias=self.eps_t[:, 0:1],
                                           scale=1.0 / D), reads=[ps.tok, self.eps_t.tok], writes=[rstd.tok])
        P.op("dve", lambda e: e.reciprocal(out=rstd[:, :], in_=rstd[:, :]), reads=[rstd.tok], writes=[rstd.tok])
        if out_f32 is None and tmp is not None:
            for kc in range(NK):
                tp_ = tmp[kc % 2]
                P.op("act", lambda e, kc=kc, tp_=tp_: e.activation(out=tp_[:, :], in_=xb[:, kc, :], func=AF.Copy,
                                                                   scale=self.gains[:, gi, kc:kc + 1]),
                     reads=[xb.tok, self.gains.tok], writes=[tp_.tok])
                P.op("pool", lambda e, kc=kc, tp_=tp_: e.tensor_tensor(out=hT[:, kc, :], in0=tp_[:, :], in1=rstd[:, :],
                                                                       op=ALU.mult),
                     reads=[tp_.tok, rstd.tok], writes=[hT.tok])
            return
        dst = hT if out_f32 is None else out_f32
        for kc in range(NK):
            P.op("dve",
                 lambda e, kc=kc: e.scalar_tensor_tensor(out=dst[:, kc, :], in0=xb[:, kc, :],
                                                         scalar=self.gains[:, gi, kc:kc + 1], in1=rstd[:, :],
                                                         op0=ALU.mult, op1=ALU.mult),
                 reads=[xb.tok, rstd.tok, self.gains.tok], writes=[dst.tok])

    def alloc_half(self, name):
        A = self.A
        return {"g": A.alloc([128, NK, FF // 2], BF16, name + "g"), "u": A.alloc([128, NK, FF // 2], BF16, name + "u"),
                "d": A.alloc([128, NF // 2, D], BF16, name + "d")}

    def load_ffn_half(self, W, l, which, hf, defer=False):
        P = self.P
        wg_d = self.w_ffn["ffn%d_w_gate" % which][l].rearrange("(c p) f -> p c f", p=128)
        wu_d = self.w_ffn["ffn%d_w_up" % which][l].rearrange("(c p) f -> p c f", p=128)
        wd_d = self.w_ffn["ffn%d_w_down" % which][l].rearrange("(c p) n -> p c n", p=128)
        fsl = slice(hf * (FF // 2), (hf + 1) * (FF // 2))
        th = []
        for kc in range(NK):
            for (key, w_d) in (("g", wg_d), ("u", wu_d)):
                th.append(lambda key=key, w_d=w_d, kc=kc: P.dma(
                    "pool", lambda e: e.dma_start(out=W[key][:, kc, :], in_=w_d[:, kc, fsl]), writes=[W[key].tok]))
        for fc in range(NF // 2):
            th.append(lambda fc=fc: P.dma(
                "pool", lambda e: e.dma_start(out=W["d"][:, fc, :], in_=wd_d[:, hf * (NF // 2) + fc, :]),
                writes=[W["d"].tok]))
        if defer:
            return th
        for f in th:
            f()
        return []

    def ffn_phase(self, l, which, src, dst, nxt):
        P, A = self.P, self.A
        A.reset(self.mark_wb)
        mk = A.mark()
        WA = self.WA
        WB = self.alloc_half("wb")
        gi = l * 3 + (0 if which == 1 else 2)
        wq = self.load_ffn_half(WB, l, which, 1, defer=True)
        if not self.precast_done:
            self.precast_done = True
            for ll in range(self.cfg.get("layers", L)):
                for kc in range(NK):
                    rs = slice(kc * 128, (kc + 1) * 128)
                    wq.append(lambda ll=ll, rs=rs: P.dma("pool", lambda e: e.dma_start(
                        out=self.WINB[ll, rs, :], in_=self.w_in[ll, rs, :]), writes=[self.tok_winb[ll]]))
                for kc in range(NK):
                    rs = slice(kc * 128, (kc + 1) * 128)
                    wq.append(lambda ll=ll, rs=rs: P.dma("pool", lambda e: e.dma_start(
                        out=self.WOUTB[ll, rs, :], in_=self.w_out[ll, rs, :]), writes=[self.tok_woutb[ll]]))

        def issue_w(n):
            for _ in range(n):
                if wq:
                    wq.pop(0)()
        xb = A.alloc([128, NK, TB], F32, "xb")
        sq = A.alloc([128, NK, TB], BF16, "sq")
        hTs = [A.alloc([128, NK, TB], BF16, "hT%d" % i) for i in range(2)]
        aT = A.alloc([128, NF // 2, TB], BF16, "aT")
        rstd = A.alloc([128, TB], F32, "rstd")
        sg = [A.alloc([128, TB], BF16, "sg%d" % i) for i in range(2)]
        stg = [A.alloc([128, TB], F32, "stg%d" % i) for i in range(4)]
        ntmp = [A.alloc([128, TB], F32, "ntmp%d" % i) for i in range(2)]
        nblk = self.cfg.get("nblk", NBLK)
        HTv = self.HT.rearrange("(c p) t -> p c t", p=128)
        nst = [0]
        tokHT = [Tok() for _ in range(nblk)]
        tokX = [[Tok() for _ in range(NK)] for _ in range(nblk)]
        for ps_ in range(2):
            W = WA if ps_ == 0 else WB
            rsrc = src if ps_ == 0 else dst
            srcv = rsrc.rearrange("(c p) t -> p c t", p=128)
            if ps_ == 1:
                issue_w(len(wq))
                if nxt is not None:
                    wq.extend(self.load_ffn_half(WA, nxt[0], nxt[1], 0, defer=True))

            def prep(blk):
                cs = slice(blk * TB, (blk + 1) * TB)
                hT = hTs[blk % 2]
                if ps_ == 0:
                    P.dma("sp" if blk == 0 else "act", lambda e, srcv=srcv, cs=cs: e.dma_start(
                        out=xb[:, :, :], in_=srcv[:, :, cs]), writes=[xb.tok])
                else:
                    P.dma("sp", lambda e, hT=hT, cs=cs: e.dma_start(out=hT[:, :, :], in_=HTv[:, :, cs]),
                          reads=[tokHT[blk]], writes=[hT.tok])

            def norm(blk, part="ab"):
                cs = slice(blk * TB, (blk + 1) * TB)
                hT = hTs[blk % 2]
                if ps_ == 0:
                    self.rms_block(xb, sq, hT, rstd, gi, tmp=ntmp, part=part)
                    if "b" not in part:
                        return
                    htq.append(lambda hT=hT, cs=cs, blk=blk: P.dma(
                        "sp", lambda e: e.dma_start(out=HTv[:, :, cs], in_=hT[:, :, :]),
                        reads=[hT.tok], writes=[tokHT[blk]]))

            htq = []
            prep(0)
            norm(0)
            if ps_ == 0 and nblk > 1:
                prep(1)
            for blk in range(nblk):
                cs = slice(blk * TB, (blk + 1) * TB)
                hT = hTs[blk % 2]
                while htq:
                    htq.pop(0)()
                if ps_ == 1 and blk + 1 < nblk:
                    prep(blk + 1)
                issue_w(8 if len(wq) > 32 else 5)
                for f in range(NF // 2):
                    pg, pu = self.ps[f % 2], self.ps[2 + f % 2]
                    fs = slice(f * 128, (f + 1) * 128)
                    for (pp, key) in ((pg, "g"), (pu, "u")):
                        for kc in range(NK):
                            P.op("pe", lambda e, pp=pp, key=key, kc=kc, fs=fs, hT=hT, W=W: e.matmul(
                                pp[:, :], lhsT=W[key][:, kc, fs], rhs=hT[:, kc, :], start=(kc == 0),
                                stop=(kc == NK - 1)), reads=[W[key].tok, hT.tok], writes=[pp.tok])
                    s_ = sg[f % 2]
                    P.op("act", lambda e, s_=s_, pg=pg: e.activation(out=s_[:, :], in_=pg[:, :], func=AF.Silu),
                         reads=[pg.tok], writes=[s_.tok])
                    P.op("dve", lambda e, s_=s_, pu=pu, f=f: e.tensor_tensor(out=aT[:, f, :], in0=pu[:, :],
                                                                             in1=s_[:, :], op=ALU.mult),
                         reads=[pu.tok, s_.tok], writes=[aT.tok])
                def ldres(n, cs=cs, rsrc=rsrc, blk=blk):
                    st = stg[(nst[0] + n) % 4]
                    P.dma("sp", lambda e, st=st, n=n, cs=cs, rsrc=rsrc: e.dma_start(
                        out=st[:, :], in_=rsrc[n * 128:(n + 1) * 128, cs]),
                        reads=([tokX[blk][n]] if ps_ == 1 else []), writes=[st.tok])

                ldres(0)
                ldres(1)
                for n in range(NK):
                    py = self.ps[4 + n % 2]
                    ns = slice(n * 128, (n + 1) * 128)
                    if n + 2 < NK:
                        ldres(n + 2)
                    if n == 0 and blk + 1 < nblk:
                        norm(blk + 1, part="a")
                    if n == 2 and blk + 1 < nblk:
                        norm(blk + 1, part="b")
                        if ps_ == 0 and blk + 2 < nblk:
                            prep(blk + 2)
                    for f in range(NF // 2):
                        P.op("pe", lambda e, py=py, f=f, ns=ns, W=W: e.matmul(
                            py[:, :], lhsT=W["d"][:, f, ns], rhs=aT[:, f, :], start=(f == 0),
                            stop=(f == NF // 2 - 1)), reads=[W["d"].tok, aT.tok], writes=[py.tok])
                    st = stg[(nst[0] + n) % 4]
                    P.op("dve", lambda e, py=py, st=st: e.scalar_tensor_tensor(
                        out=st[:, :], in0=py[:, :], scalar=0.5, in1=st[:, :], op0=ALU.mult, op1=ALU.add),
                        reads=[py.tok, st.tok], writes=[st.tok])
                    P.dma("sp", lambda e, st=st, ns=ns, cs=cs: e.dma_start(out=dst[ns, cs], in_=st[:, :]),
                          reads=[st.tok], writes=[tokX[blk][n]])
                nst[0] += NK
            issue_w(len(wq))
            if ps_ == 1:
                P.barrier()
        A.reset(mk)

    def final_phase(self, src):
        P, A = self.P, self.A
        A.reset(self.mark_wb)
        mk = A.mark()
        xbs = [A.alloc([128, NK, TB], F32, "fxb%d" % i) for i in range(2)]
        obs = [A.alloc([128, NK, TB], F32, "fob%d" % i) for i in range(2)]
        sq = A.alloc([128, NK, TB], BF16, "sq")
        rstd = A.alloc([128, TB], F32, "rstd")
        srcv = src.rearrange("(c p) t -> p c t", p=128)
        dstv = self.yT.rearrange("(c p) t -> p c t", p=128)
        nblk = self.cfg.get("nblk", NBLK)

        def ld(blk):
            cs = slice(blk * TB, (blk + 1) * TB)
            xb = xbs[blk % 2]
            P.dma("sp", lambda e: e.dma_start(out=xb[:, :, :], in_=srcv[:, :, cs]), writes=[xb.tok])

        ld(0)
        for blk in range(nblk):
            cs = slice(blk * TB, (blk + 1) * TB)
            if blk + 1 < nblk:
                ld(blk + 1)
            xb, ob = xbs[blk % 2], obs[blk % 2]
            self.rms_block(xb, sq, None, rstd, L * 3, out_f32=ob)
            P.dma("sp", lambda e, cs=cs, ob=ob: e.dma_start(out=dstv[:, :, cs], in_=ob[:, :, :]), reads=[ob.tok])
        P.barrier()
        A.reset(mk)

    def build(self):
        self.declare()
        self.setup_consts()
        cfg = self.cfg
        cur = self.xT
        nl = cfg.get("layers", L)
        ffns = []
        for l in range(nl):
            ffns.append((l, 1))
            if cfg.get("ffn2", True):
                ffns.append((l, 2))
        fi = 0
        self.load_ffn_half(self.WA, 0, 1, 0)
        for l in range(nl):
            nxt = ffns[fi + 1] if fi + 1 < len(ffns) else None
            self.ffn_phase(l, 1, cur, self.XT, nxt)
            fi += 1
            cur = self.XT
            if cfg.get("mixer", True):
                self.mixer(l, cur)
            if cfg.get("ffn2", True):
                nxt = ffns[fi + 1] if fi + 1 < len(ffns) else None
                self.ffn_phase(l, 2, cur, self.XT, nxt)
                fi += 1
        self.final_phase(cur)
        self.P.finish()
        self.P.emit()
        return self.nc

    def mixer(self, l, src):
        P, A = self.P, self.A
        A.reset(self.mark_wb)
        mk0 = A.mark()
        self.negselT = A.alloc([128, T], BF16, "negselT")
        self.proj_phase(l, src)
        stages = self.cfg.get("stages", ("fox", "nsa", "swa", "dil"))
        if "fox" in stages:
            self.fox_phase(l)
        if "nsa" in stages:
            self.nsa_cmp_phase(l)
            self.nsa_band_phase(l, "slc")
            self.nsa_band_phase(l, "win")
        if "swa" in stages:
            self.swa_phase(l)
        if "dil" in stages:
            self.dil_phase(l)
        self.outproj_phase(l, src)
        A.reset(mk0)

    def proj_phase(self, l, src):
        P, A = self.P, self.A
        mk = A.mark()
        win = A.alloc([128, NK, NCOLS], BF16, "win")
        wv = self.WINB[l].rearrange("(c p) n -> p c n", p=128)
        pieces = [(0, 896), (896, 1792), (1792, 2688), (2688, NCOLS)]
        wtok = [Tok() for _ in pieces]
        def load_win():
            for pi, (a, b) in enumerate(pieces):
                P.dma("act", lambda e, a=a, b=b: e.dma_start(out=win[:, :, a:b], in_=wv[:, :, a:b]),
                      reads=[self.tok_winb[l]], writes=[wtok[pi]])
        xb_ = A.alloc([128, NK, TB], F32, "xb")
        xbs = [xb_, xb_]
        sq = A.alloc([128, NK, TB], BF16, "sq")
        hTs = [A.alloc([128, NK, TB], BF16, "hT%d" % i) for i in range(2)]
        rstd = A.alloc([128, TB], F32, "rstd")
        zsl = [A.alloc([128, TB], BF16, "zsl%d" % i) for i in range(12)]
        gsl = [A.alloc([128, TB], F32, "gsl%d" % i) for i in range(2)]
        zva = [A.alloc([128, 4, 512], BF16, "zva%d" % i) for i in range(2)]
        zvb = [A.alloc([128, 4, 256], BF16, "zvb%d" % i) for i in range(2)]
        zf = [A.alloc([128, 4, 16], F32, "zf%d" % i) for i in range(2)]
        ntmp = [A.alloc([128, TB], F32, "ntmp%d" % i) for i in range(2)]
        srcv = src.rearrange("(c p) t -> p c t", p=128)
        ZTv = self.ZT.rearrange("(c p) t -> p c t", p=128)
        ZVAv = self.ZVA.rearrange("(s p) c -> p s c", p=128)
        ZVBv = self.ZVB.rearrange("(s p) c -> p s c", p=128)
        ZFv = self.ZF.rearrange("(s p) c -> p s c", p=128)
        gi = l * 3 + 1
        nblk = self.cfg.get("nblk", NBLK)

        def ldx(blk):
            cs = slice(blk * TB, (blk + 1) * TB)
            xb = xbs[blk % 2]
            P.dma("sp" if blk == 0 else "act", lambda e: e.dma_start(out=xb[:, :, :], in_=srcv[:, :, cs]),
                  writes=[xb.tok])

        ldx(0)
        load_win()
        self.rms_block(xbs[0], sq, hTs[0], rstd, gi, tmp=ntmp)
        if nblk > 1:
            ldx(1)
        nz = 0
        for blk in range(nblk):
            cs = slice(blk * TB, (blk + 1) * TB)
            hT = hTs[blk % 2]
            for c in range(NFM):
                pp = self.ps[c % 4]
                if c == 4 and blk + 1 < nblk:
                    self.rms_block(xbs[(blk + 1) % 2], sq, hTs[(blk + 1) % 2], rstd, gi, tmp=ntmp, part="a")
                if c == 8 and blk + 1 < nblk:
                    self.rms_block(xbs[(blk + 1) % 2], sq, hTs[(blk + 1) % 2], rstd, gi, tmp=ntmp, part="b")
                    if blk + 2 < nblk:
                        ldx(blk + 2)
                for kc in range(NK):
                    P.op("pe", lambda e, pp=pp, kc=kc, c=c, hT=hT: e.matmul(
                        pp[:, :], lhsT=win[:, kc, c * 128:(c + 1) * 128], rhs=hT[:, kc, :],
                        start=(kc == 0), stop=(kc == NK - 1)), reads=[wtok[c // 7], hT.tok], writes=[pp.tok])
                if c < 17:
                    zs = zsl[nz % 12]
                    nz += 1
                    if c % 2 == 0 and not (8 <= c <= 14 and blk + 1 < nblk):
                        P.op("act", lambda e, pp=pp, zs=zs: e.activation(out=zs[:, :], in_=pp[:, :], func=AF.Copy),
                             reads=[pp.tok], writes=[zs.tok])
                    else:
                        P.op("dve", lambda e, pp=pp, zs=zs: e.tensor_copy(out=zs[:, :], in_=pp[:, :]),
                             reads=[pp.tok], writes=[zs.tok])
                    P.dma("sp", lambda e, zs=zs, c=c, cs=cs: e.dma_start(
                        out=self.ZT[c * 128:(c + 1) * 128, cs], in_=zs[:, :]), reads=[zs.tok])
                else:
                    gs_ = gsl[c % 2]
                    P.op("act", lambda e, pp=pp, gs_=gs_: e.activation(out=gs_[:, :], in_=pp[:, :],
                                                                       func=AF.Sigmoid), reads=[pp.tok], writes=[gs_.tok])
                    P.dma("sp", lambda e, gs_=gs_, c=c, cs=cs: e.dma_start(
                        out=self.GT[(c - 17) * 128:(c - 16) * 128, cs], in_=gs_[:, :]), reads=[gs_.tok])
            za, zb, zf_ = zva[blk % 2], zvb[blk % 2], zf[blk % 2]
            for sub in range(4):
                ss = slice(sub * 128, (sub + 1) * 128)
                pa = self.ps[4 + sub % 2]
                pb = self.ps[6]
                for kc in range(NK):
                    P.op("pe", lambda e, pa=pa, kc=kc, ss=ss, hT=hT: e.matmul(
                        pa[:, :], lhsT=hT[:, kc, ss], rhs=win[:, kc, 2688:3200], start=(kc == 0), stop=(kc == NK - 1)),
                        reads=[wtok[3], hT.tok], writes=[pa.tok])
                P.op("act", lambda e, pa=pa, sub=sub, za=za: e.activation(out=za[:, sub, :], in_=pa[:, :], func=AF.Copy),
                     reads=[pa.tok], writes=[za.tok])
                for kc in range(NK):
                    P.op("pe", lambda e, pb=pb, kc=kc, ss=ss, hT=hT: e.matmul(
                        pb[:, 0:272], lhsT=hT[:, kc, ss], rhs=win[:, kc, 3200:3472], start=(kc == 0),
                        stop=(kc == NK - 1)), reads=[wtok[3], hT.tok], writes=[pb.tok])
                P.op("dve", lambda e, pb=pb, sub=sub, zb=zb: e.tensor_copy(out=zb[:, sub, :], in_=pb[:, 0:256]),
                     reads=[pb.tok], writes=[zb.tok])
                P.op("dve", lambda e, pb=pb, sub=sub, zf_=zf_: e.tensor_copy(out=zf_[:, sub, :], in_=pb[:, 256:272]),
                     reads=[pb.tok], writes=[zf_.tok])
            bs = slice(blk * 4, (blk + 1) * 4)
            P.dma("sp", lambda e, bs=bs, za=za: e.dma_start(out=ZVAv[:, bs, :], in_=za[:, :, :]), reads=[za.tok])
            P.dma("sp", lambda e, bs=bs, zb=zb: e.dma_start(out=ZVBv[:, bs, :], in_=zb[:, :, :]), reads=[zb.tok])
            P.dma("sp", lambda e, bs=bs, zf_=zf_: e.dma_start(out=ZFv[:, bs, :], in_=zf_[:, :, :]), reads=[zf_.tok])
        P.barrier()
        A.reset(mk)

    def attn_env(self):
        A = self.A
        env = {}
        env["sb"] = [self.ps[0], self.ps[1], self.ps[4], self.ps[5], self.ps[6]]
        env["ob"] = [self.ps[2], self.ps[3]]
        env["pslot"] = [A.alloc([128, 512], BF16, "pslot%d" % i) for i in range(8)]
        env["dcp"] = [A.alloc([64, 512], F32, "dcp%d" % i) for i in range(2)]
        env["osb"] = [A.alloc([64, 512], BF16, "osb%d" % i) for i in range(2)]
        env["o32"] = [A.alloc([64, 512], F32, "o32%d" % i) for i in range(2)]
        env["sc"] = 0
        env["oc"] = 0
        env["pc"] = 0
        env["fc"] = 0
        env["pending"] = []
        env["la"] = 4
        return env

    def attn_flush(self, env, keep=0):
        while len(env["pending"]) > keep:
            f = env["pending"].pop(0)
            f()

    def attn_job(self, env, qT, NQ, nblocks, tiles_fn, fin_fn, ogroup=1):
        P = self.P
        ob = None
        for I in range(nblocks):
            tl = tiles_fn(I)
            if I % ogroup == 0:
                ob = env["ob"][env["oc"] % 2]
                env["oc"] += 1
            o0 = (I % ogroup) * NQ
            qs = slice(I * NQ, (I + 1) * NQ)
            ntl = len(tl)
            for idx, t in enumerate(tl):
                sb = env["sb"][env["sc"] % 5]
                env["sc"] += 1
                psl = env["pslot"][env["pc"] % 8]
                env["pc"] += 1
                adds = t.get("adds", [])
                c0, c1 = t.get("cr", (0, NQ))
                qcs = slice(I * NQ + c0, I * NQ + c1)
                P.op("pe", lambda e, sb=sb, t=t, qcs=qcs, adds=adds, c0=c0, c1=c1: e.matmul(
                    sb[:, c0:c1], lhsT=t["k"], rhs=qT[:, qcs], start=True, stop=(len(adds) == 0)),
                    reads=[qT.tok] + t["ktoks"], writes=[sb.tok])
                for ai, (lh, rh, toks) in enumerate(adds):
                    P.op("pe", lambda e, sb=sb, lh=lh, rh=rh, ai=ai, adds=adds, c0=c0, c1=c1: e.matmul(
                        sb[:, c0:c1], lhsT=lh, rhs=rh[:, c0:c1], start=False, stop=(ai == len(adds) - 1)),
                        reads=toks, writes=[sb.tok])
                P.op("act", lambda e, sb=sb, psl=psl, t=t, c0=c0, c1=c1: e.activation(
                    out=psl[:, c0:c1], in_=sb[:, c0:c1], func=AF.Exp, bias=t["bias"], scale=0.125),
                    reads=[sb.tok] + t["btoks"], writes=[psl.tok])
                if t.get("mult") is not None:
                    mp, mtok = t["mult"]
                    env["mc"] = env.get("mc", 0) + 1
                    P.op("pool" if env["mc"] % 2 == 0 else "dve", lambda e, psl=psl, mp=mp, c0=c0, c1=c1: e.tensor_tensor(
                        out=psl[:, c0:c1], in0=psl[:, c0:c1], in1=mp[:, c0:c1], op=ALU.mult),
                        reads=[psl.tok, mtok], writes=[psl.tok])

                def pv(ob=ob, t=t, psl=psl, idx=idx, ntl=ntl, I=I, o0=o0, c0=c0, c1=c1):
                    P.op("pe", lambda e: e.matmul(ob[:, o0 + c0:o0 + c1], lhsT=t["v"], rhs=psl[:, c0:c1],
                                                  start=(idx == 0), stop=(idx == ntl - 1)),
                         reads=[psl.tok] + t["vtoks"], writes=[ob.tok])
                    if idx == ntl - 1 and (I % ogroup == ogroup - 1):
                        fin_fn(I // ogroup, ob)

                env["pending"].append(pv)
                self.attn_flush(env, keep=env["la"])

    def fin_norm(self, env, ob, NQ, hook=None, out32=False):
        P = self.P
        k = env["fc"] % 2
        env["fc"] += 1
        dcp = env["dcp"][k]
        if env.get("act_recip"):
            if hook is not None:
                P.op("dve", lambda e: e.tensor_copy(out=dcp[:, 0:NQ], in_=ob[64:128, 0:NQ]), reads=[ob.tok],
                     writes=[dcp.tok])
                hook(dcp)
                P.op("act", lambda e: e.activation(out=dcp[:, 0:NQ], in_=dcp[:, 0:NQ], func=AF.Ln), reads=[dcp.tok],
                     writes=[dcp.tok])
            else:
                P.op("act", lambda e: e.activation(out=dcp[:, 0:NQ], in_=ob[64:128, 0:NQ], func=AF.Ln), reads=[ob.tok],
                     writes=[dcp.tok])
            P.op("act", lambda e: e.activation(out=dcp[:, 0:NQ], in_=dcp[:, 0:NQ], func=AF.Exp, scale=-1.0),
                 reads=[dcp.tok], writes=[dcp.tok])
        else:
            if env.get("act_copy"):
                P.op("act", lambda e: e.activation(out=dcp[:, 0:NQ], in_=ob[64:128, 0:NQ], func=AF.Copy),
                     reads=[ob.tok], writes=[dcp.tok])
            else:
                P.op("dve", lambda e: e.tensor_copy(out=dcp[:, 0:NQ], in_=ob[64:128, 0:NQ]), reads=[ob.tok],
                     writes=[dcp.tok])
            if hook is not None:
                hook(dcp)
            P.op("dve", lambda e: e.reciprocal(out=dcp[:, 0:NQ], in_=dcp[:, 0:NQ]), reads=[dcp.tok], writes=[dcp.tok])
        dst = env["o32"][k] if out32 else env["osb"][k]
        P.op("dve", lambda e: e.tensor_tensor(out=dst[:, 0:NQ], in0=ob[0:64, 0:NQ], in1=dcp[:, 0:NQ], op=ALU.mult),
             reads=[ob.tok, dcp.tok], writes=[dst.tok])
        return dst

    def head_bufs(self, n=2):
        A, P = self.A, self.P
        sets = []
        for i in range(n):
            s = {"q": A.alloc([128, T], BF16, "qT%d" % i),
                 "kt": A.alloc([128, T], BF16, "kpt%d" % i),
                 "kb": A.alloc([128, T], BF16, "kpb%d" % i),
                 "v": A.alloc([128, NT, 128], BF16, "vaug%d" % i)}
            P.op("pool", lambda e, s=s: e.memset(s["kt"][64:128, :], 0.0), writes=[s["kt"].tok])
            P.op("dve", lambda e, s=s: e.memset(s["kb"][0:64, :], 0.0), writes=[s["kb"].tok])
            P.op("pool" if i % 2 else "dve", lambda e, s=s: e.memset(s["v"][:, :, 64:128], 1.0), writes=[s["v"].tok])
            sets.append(s)
        return sets

    def load_q(self, buf, chunk):
        self.P.dma("sp", lambda e: e.dma_start(out=buf[:, :], in_=self.ZT[chunk * 128:(chunk + 1) * 128, 0:T]),
                   writes=[buf.tok])

    def load_k(self, buf, chunk, src_half, dst_half):
        r0 = chunk * 128 + src_half * 64
        self.P.dma("sp", lambda e: e.dma_start(out=buf[dst_half * 64:(dst_half + 1) * 64, :],
                                               in_=self.ZT[r0:r0 + 64, 0:T]), writes=[buf.tok])

    def load_v(self, buf, srcv, col):
        self.P.dma("sp", lambda e: e.dma_start(out=buf[:, :, 0:64], in_=srcv[:, :, col:col + 64]), writes=[buf.tok])

    def cload(self, name, shape, dtype):
        b = self.A.alloc(shape, dtype, name)
        self.load(b, self.cin[name])
        return b

    def store_ot(self, dst_tile, NQ, row0, I):
        self.P.dma("sp", lambda e: e.dma_start(out=self.OT[row0:row0 + 64, I * NQ:(I + 1) * NQ],
                                               in_=dst_tile[:, 0:NQ]), reads=[dst_tile.tok])

    def fox_phase(self, l):
        P, A = self.P, self.A
        mk = A.mark()
        env = self.attn_env()
        hb = self.head_bufs(2)
        mc = self.cload("m_causal", [128, 4, 512], BF16)
        tri = self.cload("tri_f", [128, 128], F32)
        onesf = self.cload("ones_f", [128, 128], F32)
        zfs = A.alloc([128, NT, 16], F32, "zfs")
        self.load(zfs, self.ZF.rearrange("(s p) c -> p s c", p=128))
        bF = A.alloc([128, NT, 4], F32, "bF")
        self.load(bF, self.fox_b[l].rearrange("p (s h) -> p s h", h=4))
        lf = A.alloc([128, NT, 4], F32, "lf")
        P.op("dve", lambda e: e.tensor_tensor(out=lf[:, :, :], in0=zfs[:, :, 0:4], in1=bF[:, :, :], op=ALU.add),
             reads=[zfs.tok, bF.tok], writes=[lf.tok])
        P.op("act", lambda e: e.activation(out=lf[:, :, :], in_=lf[:, :, :], func=AF.Sigmoid), reads=[lf.tok],
             writes=[lf.tok])
        P.op("act", lambda e: e.activation(out=lf[:, :, :], in_=lf[:, :, :], func=AF.Ln), reads=[lf.tok],
             writes=[lf.tok])
        pc, pt = self.ps[4], self.ps[5]
        lf2 = lf[:, :, :].rearrange("p s h -> p (s h)")
        P.op("pe", lambda e: e.matmul(pc[:, 0:128], lhsT=tri[:, :], rhs=lf2, start=True, stop=True),
             reads=[tri.tok, lf.tok], writes=[pc.tok])
        P.op("pe", lambda e: e.matmul(pt[:, 0:128], lhsT=onesf[:, :], rhs=lf2, start=True, stop=True),
             reads=[onesf.tok, lf.tok], writes=[pt.tok])
        offs = A.alloc([128, NT + 1, 4], F32, "offs")
        P.op("dve", lambda e: e.memset(offs[:, 0, :], 0.0), writes=[offs.tok])
        for j in range(1, NT + 1):
            P.op("dve", lambda e, j=j: e.tensor_tensor(out=offs[:, j, :], in0=offs[:, j - 1, :],
                                                       in1=pt[:, (j - 1) * 4:j * 4], op=ALU.add),
                 reads=[offs.tok, pt.tok], writes=[offs.tok])
        cf = A.alloc([128, NT, 4], F32, "cf")
        P.op("dve", lambda e: e.tensor_tensor(out=cf[:, :, :], in0=offs[:, 0:NT, :],
                                              in1=pc[:, 0:128].rearrange("p (s h) -> p s h", h=4), op=ALU.add),
             reads=[offs.tok, pc.tok], writes=[cf.tok])
        bias = A.alloc([128, NBLK, NT, 4], F32, "biasF")
        for I in range(NBLK):
            J = 4 * I + 4
            for h in range(4):
                P.op("dve", lambda e, I=I, J=J, h=h: e.tensor_scalar(
                    out=bias[:, I, 0:J, h], in0=cf[:, 0:J, h], scalar1=-1.0, scalar2=offs[:, J, h:h + 1],
                    op0=ALU.mult, op1=ALU.add), reads=[cf.tok, offs.tok], writes=[bias.tok])
        ZVAv = self.ZVA.rearrange("(s p) c -> p s c", p=128)
        for h in range(4):
            s = hb[h % 2]
            half = h % 2
            kp = s["kt"] if half == 0 else s["kb"]
            self.load_q(s["q"], 0 + h // 2)
            self.load_k(kp, 2 + h // 2, half, half)
            self.load_v(s["v"], ZVAv, h * 64)

            def tiles(I, s=s, kp=kp, h=h):
                tl = []
                for j in range(4 * I + 4):
                    t = {"k": kp[:, j * 128:(j + 1) * 128], "ktoks": [kp.tok], "v": s["v"][:, j, :],
                         "vtoks": [s["v"].tok], "bias": bias[:, I, j, h:h + 1], "btoks": [bias.tok]}
                    if j >= 4 * I:
                        t["adds"] = [(self.ident_bf[:, :], mc[:, j - 4 * I, :], [self.ident_bf.tok, mc.tok])]
                        t["cr"] = (128 * (j - 4 * I), 512)
                    tl.append(t)
                return tl

            def fin(I, ob, h=h):
                d = self.fin_norm(env, ob, 512)
                self.store_ot(d, 512, h * 64, I)

            self.attn_job(env, s["q"], 512, self.cfg.get("nblk", NBLK), tiles, fin)
        self.attn_flush(env)
        P.barrier()
        A.reset(mk)

    def swa_phase(self, l):
        P, A = self.P, self.A
        mk = A.mark()
        env = self.attn_env()
        env["act_copy"] = True
        hb = self.head_bufs(2)
        ms = self.cload("m_swa", [128, 2, 128], BF16)
        bs = self.cload("b_swa", [128, 4, 2], F32)
        fs = self.cload("f_swa", [64, 4, 512], F32)
        es = A.alloc([64, 4], F32, "esink")
        self.load(es, self.sinks[l])
        P.op("act", lambda e: e.activation(out=es[:, :], in_=es[:, :], func=AF.Exp), reads=[es.tok], writes=[es.tok])
        ZVAv = self.ZVA.rearrange("(s p) c -> p s c", p=128)
        nb = self.cfg.get("nblk", NBLK) * 4
        for h in range(4):
            s = hb[h % 2]
            half = h % 2
            kv = h // 2
            kp = s["kt"] if half == 0 else s["kb"]
            self.load_q(s["q"], 10 + h // 2)
            self.load_k(kp, 12, kv, half)
            self.load_v(s["v"], ZVAv, 384 + kv * 64)

            def tiles(I, s=s, kp=kp, h=h):
                tl = []
                for ri, r in enumerate((-1, 0)):
                    j = I + r
                    if j < 0:
                        continue
                    tl.append({"k": kp[:, j * 128:(j + 1) * 128], "ktoks": [kp.tok], "v": s["v"][:, j, :],
                               "vtoks": [s["v"].tok], "bias": bs[:, h, ri:ri + 1], "btoks": [bs.tok],
                               "adds": [(self.ident_bf[:, :], ms[:, ri, :], [self.ident_bf.tok, ms.tok])]})
                return tl

            def fin(I, ob, h=h):
                def hook(dcp):
                    P.op("dve", lambda e: e.scalar_tensor_tensor(
                        out=dcp[:, 0:512], in0=fs[:, h, :], scalar=es[:, h:h + 1], in1=dcp[:, 0:512],
                        op0=ALU.mult, op1=ALU.add), reads=[fs.tok, es.tok, dcp.tok], writes=[dcp.tok])
                d = self.fin_norm(env, ob, 512, hook=hook)
                self.store_ot(d, 512, (8 + h) * 64, I)

            self.attn_job(env, s["q"], 128, nb, tiles, fin, ogroup=4)
        self.attn_flush(env)
        P.barrier()
        A.reset(mk)

    def dil_phase(self, l):
        P, A = self.P, self.A
        mk = A.mark()
        env = self.attn_env()
        env["act_copy"] = True
        hb = self.head_bufs(2)
        wd = self.cload("w_dil", [128, 20, 512], BF16)
        bd = self.cload("b_dil", [128, 4, 20], F32)
        ZVBv = self.ZVB.rearrange("(s p) c -> p s c", p=128)
        for h in range(4):
            s = hb[h % 2]
            half = h % 2
            kp = s["kt"] if half == 0 else s["kb"]
            self.load_q(s["q"], 13 + h // 2)
            self.load_k(kp, 15 + h // 2, half, half)
            self.load_v(s["v"], ZVBv, h * 64)

            def tiles(I, s=s, kp=kp, h=h):
                tl = []
                for ri, r in enumerate(range(-16, 4)):
                    j = 4 * I + r
                    if j < 0:
                        continue
                    tl.append({"k": kp[:, j * 128:(j + 1) * 128], "ktoks": [kp.tok], "v": s["v"][:, j, :],
                               "vtoks": [s["v"].tok], "bias": bd[:, h, ri:ri + 1], "btoks": [bd.tok],
                               "mult": (wd[:, ri, :], wd.tok),
                               "cr": ((128 * r, 512) if r >= 0 else (0, min(512, 128 * (r + 17))))})
                return tl

            def fin(I, ob, h=h):
                d = self.fin_norm(env, ob, 512)
                self.store_ot(d, 512, (12 + h) * 64, I)

            self.attn_job(env, s["q"], 512, self.cfg.get("nblk", NBLK), tiles, fin)
        self.attn_flush(env)
        P.barrier()
        A.reset(mk)

    def nsa_band_phase(self, l, kind):
        P, A = self.P, self.A
        mk = A.mark()
        env = self.attn_env()
        env["act_copy"] = (kind == "win")
        hb = self.head_bufs(1)[0]
        q2 = A.alloc([128, T], BF16, "q2")
        qs = [hb["q"], q2]
        self.load_q(qs[0], 4)
        self.load_q(qs[1], 5)
        kch = 8 if kind == "slc" else 9
        self.load_k(hb["kt"], kch, 0, 0)
        self.load_k(hb["kb"], kch, 1, 1)
        ZVAv = self.ZVA.rearrange("(s p) c -> p s c", p=128)
        self.load_v(hb["v"], ZVAv, 256 if kind == "slc" else 320)
        if kind == "slc":
            mc = self.cload("m_causal", [128, 4, 512], BF16)
            bt = self.cload("b_slc", [128, 4, 32], F32)
            ex = self.cload("expand", [128, 32, 128], BF16)
        else:
            mw = self.cload("m_win", [128, 8, 512], BF16)
            bt = self.cload("b_win", [128, 4, 8], F32)
        gt = [A.alloc([64, 512], F32, "gt%d" % i) for i in range(2)]
        pt = [A.alloc([64, 512], F32, "pt%d" % i) for i in range(2)]
        cnt = [0]
        gbase = 0 if kind == "slc" else 2
        psrc = self.OCT if kind == "slc" else self.PART
        for h in range(4):
            half = h % 2
            kp = hb["kt"] if half == 0 else hb["kb"]
            q = qs[h // 2]

            def tiles(I, kp=kp, h=h):
                tl = []
                if kind == "slc":
                    for j in range(4 * I + 4):
                        rel = j - 4 * I
                        adds = [(ex[:, j, :], self.negselT[:, I * 512:(I + 1) * 512], [ex.tok, self.negselT.tok])]
                        if rel >= 0:
                            adds.append((self.ident_bf[:, :], mc[:, rel, :], [self.ident_bf.tok, mc.tok]))
                        tl.append({"k": kp[:, j * 128:(j + 1) * 128], "ktoks": [kp.tok], "v": hb["v"][:, j, :],
                                   "vtoks": [hb["v"].tok], "bias": bt[:, h, rel + 28:rel + 29], "btoks": [bt.tok],
                                   "adds": adds, "cr": ((128 * rel, 512) if rel >= 0 else (0, 512))})
                else:
                    for ri, r in enumerate(range(-4, 4)):
                        j = 4 * I + r
                        if j < 0:
                            continue
                        tl.append({"k": kp[:, j * 128:(j + 1) * 128], "ktoks": [kp.tok], "v": hb["v"][:, j, :],
                                   "vtoks": [hb["v"].tok], "bias": bt[:, h, ri:ri + 1], "btoks": [bt.tok],
                                   "adds": [(self.ident_bf[:, :], mw[:, ri, :], [self.ident_bf.tok, mw.tok])],
                                   "cr": ((128 * r, 512) if r >= 0 else (0, min(512, 128 * (r + 5))))})
                return tl

            def fin(I, ob, h=h):
                k = cnt[0] % 2
                cnt[0] += 1
                g, pp = gt[k], pt[k]
                cs = slice(I * 512, (I + 1) * 512)
                gr0 = (gbase + h // 2) * 128 + (h % 2) * 64
                P.dma("sp", lambda e: e.dma_start(out=g[:, :], in_=self.GT[gr0:gr0 + 64, cs]), writes=[g.tok])
                P.dma("sp", lambda e: e.dma_start(out=pp[:, :], in_=psrc[h * 64:(h + 1) * 64, cs]), writes=[pp.tok])
                d = self.fin_norm(env, ob, 512, out32=True)
                eng2 = "pool" if kind == "win" else "dve"
                P.op(eng2, lambda e: e.tensor_tensor(out=d[:, :], in0=d[:, :], in1=g[:, :], op=ALU.mult),
                     reads=[d.tok, g.tok], writes=[d.tok])
                if kind == "slc":
                    P.op("dve", lambda e: e.tensor_tensor(out=d[:, :], in0=d[:, :], in1=pp[:, :], op=ALU.add),
                         reads=[d.tok, pp.tok], writes=[d.tok])
                    P.dma("sp", lambda e: e.dma_start(out=self.PART[h * 64:(h + 1) * 64, cs], in_=d[:, :]),
                          reads=[d.tok])
                else:
                    ob16 = env["osb"][k]
                    P.op("pool", lambda e: e.tensor_tensor(out=ob16[:, :], in0=d[:, :], in1=pp[:, :], op=ALU.add),
                         reads=[d.tok, pp.tok], writes=[ob16.tok])
                    self.store_ot(ob16, 512, (4 + h) * 64, I)

            self.attn_job(env, q, 512, self.cfg.get("nblk", NBLK), tiles, fin)
        self.attn_flush(env)
        P.barrier()
        A.reset(mk)

    def nsa_cmp_phase(self, l):
        P, A = self.P, self.A
        mk = A.mark()
        q = [A.alloc([128, T], BF16, "cq%d" % i) for i in range(2)]
        self.load_q(q[0], 4)
        self.load_q(q[1], 5)
        mcmp = self.cload("m_cmp", [128, 17, 128], BF16)
        bcmp = self.cload("b_cmp", [128, 4, 48], F32)
        rcmp = self.cload("r_cmp", [128, 2, 130], BF16)
        sa = self.cload("sel_a", [128, 32, 64], F32)
        sbt = self.cload("sel_b", [128, 32, 64], F32)
        scz = self.cload("sel_c", [128, 32, 64], F32)
        identf = self.cload("ident_f", [128, 128], F32)
        gsb = A.alloc([128, NT, 16], F32, "gsbt")
        self.load(gsb, self.ZF.rearrange("(s p) c -> p s c", p=128))
        P.op("act", lambda e: e.activation(out=gsb[:, :, :], in_=gsb[:, :, :], func=AF.Sigmoid), reads=[gsb.tok],
             writes=[gsb.tok])
        kcp = [A.alloc([128, 256], BF16, "kcp%d" % i) for i in range(2)]
        P.op("pool", lambda e: e.memset(kcp[0][64:128, :], 0.0), writes=[kcp[0].tok])
        P.op("pool", lambda e: e.memset(kcp[1][0:64, :], 0.0), writes=[kcp[1].tok])
        for idx in range(2):
            kk = A.alloc([128, T + 32], BF16, "kk%d" % idx)
            P.op("pool", lambda e, kk=kk: e.memset(kk[:, T - 32:T + 32], 0.0), writes=[kk.tok])
            r0 = (6 + idx) * 128
            P.dma("sp", lambda e, kk=kk, r0=r0: e.dma_start(out=kk[0:64, 0:T], in_=self.ZT[r0:r0 + 64, 0:T]),
                  writes=[kk.tok])
            P.dma("sp", lambda e, kk=kk, r0=r0: e.dma_start(out=kk[64:128, 0:T - 1], in_=self.ZT[r0 + 64:r0 + 128, 1:T]),
                  writes=[kk.tok])
            w1 = A.alloc([128, 16, 128], BF16, "w1_%d" % idx)
            P.dma("pool", lambda e, w1=w1, idx=idx: e.dma_start(
                out=w1[:, :, :], in_=self.cmp_w1[l, idx].rearrange("(lp p) e -> p lp e", p=128)), writes=[w1.tok])
            pe = A.alloc([128, 16], BF16, "pe_%d" % idx)
            P.dma("pool", lambda e, pe=pe, idx=idx: e.dma_start(out=pe[:, :], in_=self.cmp_pe[l, idx]),
                  writes=[pe.tok])
            w2 = A.alloc([128, 128], BF16, "w2_%d" % idx)
            P.dma("pool", lambda e, w2=w2, idx=idx: e.dma_start(out=w2[:, :], in_=self.cmp_w2[l, idx]),
                  writes=[w2.tok])
            pcst, ph = self.ps[4], self.ps[5]
            for lp in range(16):
                P.op("pe", lambda e, lp=lp, w1=w1, pe=pe: e.matmul(pcst[:, 0:1], lhsT=w1[:, lp, :], rhs=pe[:, lp:lp + 1],
                                                                   start=(lp == 0), stop=(lp == 15)),
                     reads=[w1.tok, pe.tok], writes=[pcst.tok])
            cst = A.alloc([128, 1], F32, "cst%d" % idx)
            P.op("dve", lambda e, cst=cst: e.tensor_copy(out=cst[:, :], in_=pcst[:, 0:1]), reads=[pcst.tok],
                 writes=[cst.tok])
            for lp in range(16):
                P.op("pe", lambda e, lp=lp, w1=w1, kk=kk: e.matmul(
                    ph[:, 0:256], lhsT=w1[:, lp, :], rhs=kk[:, 2 * lp:2 * lp + 4096:16], start=(lp == 0),
                    stop=(lp == 15)), reads=[w1.tok, kk.tok], writes=[ph.tok])
            hid = A.alloc([128, 256], BF16, "hid%d" % idx)
            P.op("act", lambda e, hid=hid, cst=cst: e.activation(out=hid[:, :], in_=ph[:, 0:256], func=AF.Silu,
                                                                 bias=cst[:, 0:1]),
                 reads=[ph.tok, cst.tok], writes=[hid.tok])
            if idx == 0:
                pk = self.ps[6]
                P.op("pe", lambda e, w2=w2, hid=hid: e.matmul(pk[:, 0:256], lhsT=w2[:, :], rhs=hid[:, :], start=True,
                                                              stop=True), reads=[w2.tok, hid.tok], writes=[pk.tok])
                P.op("dve", lambda e: e.tensor_copy(out=kcp[0][0:64, :], in_=pk[0:64, 0:256]), reads=[pk.tok],
                     writes=[kcp[0].tok])
                P.op("dve", lambda e: e.tensor_copy(out=kcp[1][64:128, :], in_=pk[64:128, 0:256]), reads=[pk.tok],
                     writes=[kcp[1].tok])
            else:
                for ch in range(2):
                    pv = self.ps[6]
                    P.op("pe", lambda e, ch=ch, w2=w2, hid=hid: e.matmul(
                        pv[:, 0:64], lhsT=hid[:, ch * 128:(ch + 1) * 128], rhs=w2[:, 0:64], start=True, stop=True),
                        reads=[w2.tok, hid.tok], writes=[pv.tok])
                    P.op("dve", lambda e, ch=ch: e.tensor_copy(out=rcmp[:, ch, 65:129], in_=pv[:, 0:64]),
                         reads=[pv.tok], writes=[rcmp.tok])
        pslot = [A.alloc([128, 128], BF16, "cps%d" % i) for i in range(3)]
        imp = A.alloc([128, 64], F32, "imp")
        ocmp = A.alloc([128, 256], F32, "ocmp")
        rd = A.alloc([128, 4], F32, "rd")
        rd2 = A.alloc([128, 4], F32, "rd2")
        sc = A.alloc([128, 64], F32, "sc")
        sc2 = A.alloc([128, 64], F32, "sc2")
        mx = A.alloc([128, 16], F32, "mx")
        negs = A.alloc([128, 128], BF16, "negs")
        octs = A.alloc([128, 2, 128], F32, "octs")
        P.op("pool", lambda e: e.memset(negs[:, :], 0.0), writes=[negs.tok])
        OCTv = self.OCT.rearrange("(k p) t -> p k t", p=128)
        npc = 0
        nsb = 0
        sbanks = [self.ps[0], self.ps[1], self.ps[6]]
        pslot = pslot + [A.alloc([128, 128], BF16, "cps%d" % (3 + i)) for i in range(3)]
        pend = []

        def stage_b(tt, h, pC, items):
            for ci, (ch, psl) in enumerate(items):
                P.op("pe", lambda e, pC=pC, psl=psl, ch=ch, ci=ci, n=len(items): e.matmul(
                    pC[:, 0:129], lhsT=psl[:, :], rhs=rcmp[:, ch, 0:129], start=(ci == 0),
                    stop=(ci == n - 1)), reads=[psl.tok, rcmp.tok], writes=[pC.tok])
            P.op("dve", lambda e, pC=pC, h=h: e.tensor_scalar_max(out=rd[:, h:h + 1], in0=pC[:, 64:65],
                                                                  scalar1=1e-30), reads=[pC.tok], writes=[rd.tok])
            P.op("dve", lambda e, h=h: e.reciprocal(out=rd[:, h:h + 1], in_=rd[:, h:h + 1]), reads=[rd.tok],
                 writes=[rd.tok])
            if h == 0:
                P.op("dve", lambda e, pC=pC, h=h: e.tensor_scalar(out=imp[:, :], in0=pC[:, 0:64],
                                                                  scalar1=rd[:, h:h + 1], scalar2=None,
                                                                  op0=ALU.mult),
                     reads=[pC.tok, rd.tok], writes=[imp.tok])
            else:
                P.op("dve", lambda e, pC=pC, h=h: e.scalar_tensor_tensor(
                    out=imp[:, :], in0=pC[:, 0:64], scalar=rd[:, h:h + 1], in1=imp[:, :], op0=ALU.mult,
                    op1=ALU.add), reads=[pC.tok, rd.tok, imp.tok], writes=[imp.tok])
            P.op("dve", lambda e, h=h, tt=tt: e.tensor_tensor(out=rd2[:, h:h + 1], in0=rd[:, h:h + 1],
                                                              in1=gsb[:, tt, 4 + 3 * h:5 + 3 * h], op=ALU.mult),
                 reads=[rd.tok, gsb.tok], writes=[rd2.tok])
            P.op("act", lambda e, pC=pC, h=h: e.activation(out=ocmp[:, h * 64:(h + 1) * 64], in_=pC[:, 65:129],
                                                           func=AF.Copy, scale=rd2[:, h:h + 1]),
                 reads=[pC.tok, rd2.tok], writes=[ocmp.tok])
            if h == 3:
                epilogue(tt)

        def epilogue(tt):
            ts = slice(tt * 128, (tt + 1) * 128)
            for k in range(2):
                ptp = self.ps[4 + k]
                P.op("pe", lambda e, ptp=ptp, k=k: e.transpose(ptp[:, 0:128], ocmp[:, k * 128:(k + 1) * 128],
                                                               identf[:, :]),
                     reads=[ocmp.tok, identf.tok], writes=[ptp.tok])
                P.op("act", lambda e, ptp=ptp, k=k: e.activation(out=octs[:, k, :], in_=ptp[:, 0:128], func=AF.Copy),
                     reads=[ptp.tok], writes=[octs.tok])
            P.dma("sp", lambda e, ts=ts: e.dma_start(out=OCTv[:, :, ts], in_=octs[:, :, :]), reads=[octs.tok])
            P.op("dve", lambda e, tt=tt: e.tensor_tensor(out=sc[:, :], in0=imp[:, :], in1=sa[:, tt, :], op=ALU.mult),
                 reads=[imp.tok, sa.tok], writes=[sc.tok])
            P.op("dve", lambda e, tt=tt: e.tensor_tensor(out=sc[:, :], in0=sc[:, :], in1=sbt[:, tt, :], op=ALU.add),
                 reads=[sc.tok, sbt.tok], writes=[sc.tok])
            P.op("dve", lambda e: e.max(out=mx[:, 0:8], in_=sc[:, :]), reads=[sc.tok], writes=[mx.tok])
            P.op("dve", lambda e: e.match_replace(out=sc2[:, :], in_to_replace=mx[:, 0:8], in_values=sc[:, :],
                                                  imm_value=-100.0), reads=[sc.tok, mx.tok], writes=[sc2.tok])
            P.op("dve", lambda e: e.max(out=mx[:, 8:16], in_=sc2[:, :]), reads=[sc2.tok], writes=[mx.tok])
            P.op("dve", lambda e: e.tensor_scalar(out=sc2[:, :], in0=sc[:, :], scalar1=mx[:, 15:16], scalar2=None,
                                                  op0=ALU.is_ge), reads=[sc.tok, mx.tok], writes=[sc2.tok])
            P.op("dve", lambda e, tt=tt: e.tensor_tensor(out=sc2[:, :], in0=sc2[:, :], in1=scz[:, tt, :],
                                                         op=ALU.mult), reads=[sc2.tok, scz.tok], writes=[sc2.tok])
            P.op("dve", lambda e: e.tensor_scalar(out=negs[:, 0:64], in0=sc2[:, :], scalar1=-1.0, scalar2=-NEG,
                                                  op0=ALU.add, op1=ALU.mult), reads=[sc2.tok], writes=[negs.tok])
            P.op("pe", lambda e: e.transpose(self.psb[:, 0:128], negs[:, :], self.ident_bf[:, :]),
                 reads=[negs.tok, self.ident_bf.tok], writes=[self.psb.tok])
            nsT = self.negselT
            P.op("act", lambda e, ts=ts, nsT=nsT: e.activation(out=nsT[:, ts], in_=self.psb[:, 0:128], func=AF.Copy),
                 reads=[self.psb.tok], writes=[nsT.tok])

        for tt in range(self.cfg.get("nblk", NBLK) * 4):
            ts = slice(tt * 128, (tt + 1) * 128)
            for h in range(4):
                qq = q[h // 2]
                kc_ = kcp[h % 2]
                pC = self.ps[2 + h % 2]
                chs = [0] + ([1] if tt >= 16 else [])
                items = []
                for ci, ch in enumerate(chs):
                    di = tt - 16 * ch
                    sb = sbanks[nsb % 3]
                    nsb += 1
                    psl = pslot[npc % 6]
                    npc += 1
                    need_mask = di <= 16
                    P.op("pe", lambda e, sb=sb, kc_=kc_, ch=ch, qq=qq, ts=ts, need_mask=need_mask: e.matmul(
                        sb[:, 0:128], lhsT=kc_[:, ch * 128:(ch + 1) * 128], rhs=qq[:, ts], start=True,
                        stop=(not need_mask)), reads=[kc_.tok, qq.tok], writes=[sb.tok])
                    if need_mask:
                        P.op("pe", lambda e, sb=sb, di=di: e.matmul(sb[:, 0:128], lhsT=self.ident_bf[:, :],
                                                                    rhs=mcmp[:, di, :], start=False, stop=True),
                             reads=[self.ident_bf.tok, mcmp.tok], writes=[sb.tok])
                    P.op("act", lambda e, sb=sb, psl=psl, h=h, di=di: e.activation(
                        out=psl[:, :], in_=sb[:, 0:128], func=AF.Exp, bias=bcmp[:, h, di + 16:di + 17], scale=0.125),
                        reads=[sb.tok, bcmp.tok], writes=[psl.tok])
                    items.append((ch, psl))
                pend.append((tt, h, pC, items))
                if len(pend) > 2:
                    stage_b(*pend.pop(0))
        while pend:
            stage_b(*pend.pop(0))
        P.barrier()
        A.reset(mk)

    def outproj_phase(self, l, src):
        P, A = self.P, self.A
        mk = A.mark()
        wo = A.alloc([128, NK, D], BF16, "wo")
        wov = self.WOUTB[l].rearrange("(c p) n -> p c n", p=128)
        P.dma("act", lambda e: e.dma_start(out=wo[:, :, :], in_=wov[:, :, :]), reads=[self.tok_woutb[l]], writes=[wo.tok])
        xbs = [A.alloc([128, NK, TB], F32, "xb%d" % i) for i in range(2)]
        ots = [A.alloc([128, NK, TB], BF16, "otb%d" % i) for i in range(2)]
        srcv = src.rearrange("(c p) t -> p c t", p=128)
        dstv = self.XT.rearrange("(c p) t -> p c t", p=128)
        OTv = self.OT.rearrange("(c p) t -> p c t", p=128)
        nblk = self.cfg.get("nblk", NBLK)

        def ld(blk):
            cs = slice(blk * TB, (blk + 1) * TB)
            xb, ot = xbs[blk % 2], ots[blk % 2]
            P.dma("sp", lambda e: e.dma_start(out=ot[:, :, :], in_=OTv[:, :, cs]), writes=[ot.tok])
            P.dma("sp", lambda e: e.dma_start(out=xb[:, :, :], in_=srcv[:, :, cs]), writes=[xb.tok])

        ld(0)
        for blk in range(nblk):
            cs = slice(blk * TB, (blk + 1) * TB)
            xb, ot = xbs[blk % 2], ots[blk % 2]
            if blk + 1 < nblk:
                ld(blk + 1)
            for n in range(NK):
                py = self.ps[4 + n % 2]
                for mc_ in range(NK):
                    P.op("pe", lambda e, py=py, mc_=mc_, n=n, ot=ot: e.matmul(
                        py[:, :], lhsT=wo[:, mc_, n * 128:(n + 1) * 128], rhs=ot[:, mc_, :], start=(mc_ == 0),
                        stop=(mc_ == NK - 1)), reads=[wo.tok, ot.tok], writes=[py.tok])
                P.op("dve", lambda e, py=py, n=n, xb=xb: e.tensor_tensor(out=xb[:, n, :], in0=py[:, :], in1=xb[:, n, :],
                                                                  op=ALU.add), reads=[py.tok, xb.tok], writes=[xb.tok])
            P.dma("sp", lambda e, cs=cs, xb=xb: e.dma_start(out=dstv[:, :, cs], in_=xb[:, :, :]), reads=[xb.tok])
        P.barrier()
        A.reset(mk)


_COLS = None


def prep_shared(inp):
    global _COLS
    if _COLS is None:
        _COLS = win_col_layout()
    f32 = np.float32
    sh = {}
    gains = np.stack([np.stack([inp["norm_ffn1"][l], inp["norm_mix"][l], inp["norm_ffn2"][l]]) for l in range(L)])
    gains = np.concatenate([gains.reshape(L * 3, D), inp["norm_final"].reshape(1, D)], axis=0)
    sh["norms"] = np.ascontiguousarray(gains.reshape(L * 3 + 1, NK, 128).transpose(0, 2, 1)).astype(f32)
    for nm in ("ffn1_w_gate", "ffn1_w_up", "ffn2_w_gate", "ffn2_w_up", "ffn1_w_down", "ffn2_w_down", "w_out"):
        sh[nm] = np.ascontiguousarray(inp[nm], dtype=f32)
    sh["w_in_ext"] = np.ascontiguousarray(np.asarray(inp["w_in"], f32)[:, :, _COLS])
    sh["fox_b_rep"] = np.ascontiguousarray(
        np.broadcast_to(np.asarray(inp["fox_b_f"], f32)[:, None, None, :], (L, 128, NT, 4)).reshape(L, 128, 128))
    pe = np.asarray(inp["nsa_cmp_pe"], f32)
    sh["cmp_pe_l"] = np.ascontiguousarray(pe.reshape(L, 2, 16, 128).transpose(0, 1, 3, 2))
    sh["nsa_cmp_w1"] = np.ascontiguousarray(np.asarray(inp["nsa_cmp_w1"], f32).reshape(L, 2, 2048, 128))
    w2 = np.asarray(inp["nsa_cmp_w2"], f32)
    sh["cmp_w2_dup"] = np.ascontiguousarray(np.concatenate([w2, w2], axis=-1))
    sh["sinks_rep"] = np.ascontiguousarray(
        np.broadcast_to(np.asarray(inp["swa_sinks"], f32)[:, None, :], (L, 64, 4)))
    for k, v in const_specs().items():
        sh["c_" + k] = v
    return sh


_NC_CACHE = {}


def get_program(cfg_key=None, cfg=None):
    key = cfg_key or "full"
    if key not in _NC_CACHE:
        b = Builder(cfg)
        _NC_CACHE[key] = (b.build(), b)
    return _NC_CACHE[key]


def kernel(**inputs):
    x = np.asarray(inputs["x"], np.float32)
    B = x.shape[0]
    nc, b = get_program()
    sh = prep_shared(inputs)
    in_maps = []
    for c in range(B):
        m = dict(sh)
        m["xT"] = np.ascontiguousarray(x[c].T)
        in_maps.append({k: m[k] for k in b.inputs})
    res = run_bass_kernel_spmd(nc, in_maps, core_ids=list(range(B)))
    out = np.stack([np.ascontiguousarray(np.asarray(r["yT"]).T) for r in res.results], axis=0)
    return out.astype(np.float32)
```
